# Optimizing a Trainium2 kernel written in Bass

```python
import math
import jax, jax.numpy as jnp
from jax import lax
import numpy as np


D_MODEL = 2048
BATCH = 2
SEQ = 16384
DEPTH = 1
DEC_BATCH = 1
DEC_SEQ = 16384
PAST_LEN = 128

DA_HEADS = 8
DA_QK_DIM = 64
DA_V_DIM = 2 * DA_QK_DIM
MLA_HEADS = 8
MLA_NOPE_DIM = 128
MLA_ROPE_DIM = 64
MLA_V_DIM = 128
MLA_Q_RANK = 512
MLA_KV_RANK = 256
D_FF = -(-8 * D_MODEL // (3 * 256)) * 256

ROPE_THETA = 10000.0
RMS_EPS = 1e-6
Q_BLOCK = 128

DA_Q_W = DA_HEADS * 2 * DA_QK_DIM
DA_K_W = DA_HEADS * 2 * DA_QK_DIM
DA_V_W = DA_HEADS * DA_V_DIM
MLA_KR_W = MLA_ROPE_DIM
GATE_W = D_MODEL
IN_WIDTHS = (DA_Q_W, DA_K_W, DA_V_W, MLA_Q_RANK, MLA_KV_RANK, MLA_KR_W, GATE_W, GATE_W)
IN_W = sum(IN_WIDTHS)
A_WIDTH = DA_HEADS * DA_V_DIM
B_WIDTH = MLA_HEADS * MLA_V_DIM

kernel_name = 'hybrid_diffattn_mla_gated_encoder'


def _rmsnorm(x, g):
    xf = x.astype(jnp.float32)
    y = xf * lax.rsqrt(jnp.mean(xf * xf, axis=-1, keepdims=True) + RMS_EPS)
    return (y * g.astype(jnp.float32)).astype(x.dtype)


def _rope(x, pos):
    inv = ROPE_THETA ** (-jnp.arange(0, MLA_ROPE_DIM, 2, dtype=jnp.float32) / MLA_ROPE_DIM)
    ang = pos[:, None] * inv[None, :]
    cos = jnp.concatenate([jnp.cos(ang), jnp.cos(ang)], axis=-1)
    sin = jnp.concatenate([jnp.sin(ang), jnp.sin(ang)], axis=-1)
    xf = x.astype(jnp.float32)
    half = MLA_ROPE_DIM // 2
    rot = jnp.concatenate([-xf[..., half:], xf[..., :half]], axis=-1)
    return (xf * cos + rot * sin).astype(x.dtype)


def _alibi_slopes(n):
    return jnp.power(2.0, -8.0 * jnp.arange(1, n + 1, dtype=jnp.float32) / n)


def _to_blocks(t):
    b, h, s, d = t.shape
    return t.reshape(b, h, s // Q_BLOCK, Q_BLOCK, d).transpose(2, 0, 1, 3, 4)


def _from_blocks(o):
    nb, b, h, qb, d = o.shape
    return o.transpose(1, 2, 0, 3, 4).reshape(b, h, nb * qb, d)


def _diff_attention(q1, q2, k1, k2, v, lam, slopes):
    s_len = q1.shape[2]
    nb = s_len // Q_BLOCK
    scale = DA_QK_DIM ** -0.5
    k_pos = jnp.arange(s_len, dtype=jnp.float32)

    def block(args):
        q1b, q2b, bi = args
        q_pos = (bi * Q_BLOCK + jnp.arange(Q_BLOCK)).astype(jnp.float32)
        bias = -slopes[:, None, None] * jnp.abs(q_pos[:, None] - k_pos[None, :])[None]
        s1 = jnp.einsum('bhqd,bhkd->bhqk', q1b, k1).astype(jnp.float32) * scale + bias
        s2 = jnp.einsum('bhqd,bhkd->bhqk', q2b, k2).astype(jnp.float32) * scale + bias
        w = (jax.nn.softmax(s1, axis=-1) - lam * jax.nn.softmax(s2, axis=-1)).astype(v.dtype)
        return jnp.einsum('bhqk,bhkv->bhqv', w, v)

    out = lax.map(block, (_to_blocks(q1), _to_blocks(q2), jnp.arange(nb)))
    return _from_blocks(out)


def _mla_attention(q_nope, q_rope, k_nope, k_rope, v):
    s_len = q_nope.shape[2]
    scale = (MLA_NOPE_DIM + MLA_ROPE_DIM) ** -0.5

    def block(args):
        qnb, qrb = args
        s = (jnp.einsum('bhqd,bhkd->bhqk', qnb, k_nope)
             + jnp.einsum('bhqr,bkr->bhqk', qrb, k_rope)).astype(jnp.float32) * scale
        p = jax.nn.softmax(s, axis=-1).astype(v.dtype)
        return jnp.einsum('bhqk,bhkv->bhqv', p, v)

    out = lax.map(block, (_to_blocks(q_nope), _to_blocks(q_rope)))
    return _from_blocks(out)


def _encoder(x, params):
    (norm_mix, w_in, da_lambda_q1, da_lambda_k1, da_lambda_q2, da_lambda_k2, da_subln,
     mla_q_norm, w_uq, mla_kv_norm, w_ukv, w_proj_a, w_proj_b, w_out,
     norm_ffn, w_gate, w_up, w_down, norm_final) = params
    b, s_len, _ = x.shape
    pos = jnp.arange(s_len, dtype=jnp.float32)
    slopes = _alibi_slopes(DA_HEADS)
    offsets = []
    acc = 0
    for wdt in IN_WIDTHS[:-1]:
        acc += wdt
        offsets.append(acc)

    for l in range(DEPTH):
        lambda_init = 0.8 - 0.6 * math.exp(-0.3 * l)
        h = _rmsnorm(x, norm_mix[l])
        proj = h @ w_in[l]
        q_da, k_da, v_da, cq, ckv, kr, ga, gb = jnp.split(proj, offsets, axis=-1)

        q = q_da.reshape(b, s_len, DA_HEADS, 2, DA_QK_DIM).transpose(3, 0, 2, 1, 4)
        k = k_da.reshape(b, s_len, DA_HEADS, 2, DA_QK_DIM).transpose(3, 0, 2, 1, 4)
        v_a = v_da.reshape(b, s_len, DA_HEADS, DA_V_DIM).transpose(0, 2, 1, 3)
        lam = (jnp.exp(jnp.sum(da_lambda_q1[l].astype(jnp.float32) * da_lambda_k1[l].astype(jnp.float32)))
               - jnp.exp(jnp.sum(da_lambda_q2[l].astype(jnp.float32) * da_lambda_k2[l].astype(jnp.float32)))
               + lambda_init)
        o_a = _diff_attention(q[0], q[1], k[0], k[1], v_a, lam, slopes)
        o_a = _rmsnorm(o_a, da_subln[l]) * (1.0 - lambda_init)
        o_a = o_a.transpose(0, 2, 1, 3).reshape(b, s_len, A_WIDTH)

        cq = _rmsnorm(cq, mla_q_norm[l])
        qm = (cq @ w_uq[l]).reshape(b, s_len, MLA_HEADS, MLA_NOPE_DIM + MLA_ROPE_DIM).transpose(0, 2, 1, 3)
        q_nope, q_rope = qm[..., :MLA_NOPE_DIM], _rope(qm[..., MLA_NOPE_DIM:], pos)
        ckv = _rmsnorm(ckv, mla_kv_norm[l])
        kv = (ckv @ w_ukv[l]).reshape(b, s_len, MLA_HEADS, MLA_NOPE_DIM + MLA_V_DIM).transpose(0, 2, 1, 3)
        k_nope, v_b = kv[..., :MLA_NOPE_DIM], kv[..., MLA_NOPE_DIM:]
        k_rope = _rope(kr, pos)
        o_b = _mla_attention(q_nope, q_rope, k_nope, k_rope, v_b)
        o_b = o_b.transpose(0, 2, 1, 3).reshape(b, s_len, B_WIDTH)

        merged = jax.nn.sigmoid(ga) * (o_a @ w_proj_a[l]) + jax.nn.sigmoid(gb) * (o_b @ w_proj_b[l])
        x = x + merged @ w_out[l]

        h2 = _rmsnorm(x, norm_ffn[l])
        x = x + (jax.nn.silu(h2 @ w_gate[l]) * (h2 @ w_up[l])) @ w_down[l]

    return _rmsnorm(x, norm_final)


def setup_inputs(seed: int = 0) -> dict:
    key = jax.random.key(seed)
    ks = jax.random.split(key, 24)
    f32 = jnp.float32

    def nrm(k, shape, scale):
        return jax.random.normal(k, shape, f32) * scale

    def gain(k, shape):
        return 1.0 + 0.02 * jax.random.normal(k, shape, f32)

    return {
        'x_prompt': nrm(ks[0], (BATCH, SEQ, D_MODEL), 1.0),
        'x_sample': nrm(ks[1], (DEC_BATCH, DEC_SEQ, D_MODEL), 1.0),
        'norm_mix': gain(ks[2], (DEPTH, D_MODEL)),
        'w_in': nrm(ks[3], (DEPTH, D_MODEL, IN_W), D_MODEL ** -0.5),
        'da_lambda_q1': nrm(ks[4], (DEPTH, DA_QK_DIM), 0.1),
        'da_lambda_k1': nrm(ks[5], (DEPTH, DA_QK_DIM), 0.1),
        'da_lambda_q2': nrm(ks[6], (DEPTH, DA_QK_DIM), 0.1),
        'da_lambda_k2': nrm(ks[7], (DEPTH, DA_QK_DIM), 0.1),
        'da_subln': gain(ks[8], (DEPTH, DA_V_DIM)),
        'mla_q_norm': gain(ks[9], (DEPTH, MLA_Q_RANK)),
        'w_uq': nrm(ks[10], (DEPTH, MLA_Q_RANK, MLA_HEADS * (MLA_NOPE_DIM + MLA_ROPE_DIM)), MLA_Q_RANK ** -0.5),
        'mla_kv_norm': gain(ks[11], (DEPTH, MLA_KV_RANK)),
        'w_ukv': nrm(ks[12], (DEPTH, MLA_KV_RANK, MLA_HEADS * (MLA_NOPE_DIM + MLA_V_DIM)), MLA_KV_RANK ** -0.5),
        'w_proj_a': nrm(ks[13], (DEPTH, A_WIDTH, D_MODEL), A_WIDTH ** -0.5),
        'w_proj_b': nrm(ks[14], (DEPTH, B_WIDTH, D_MODEL), B_WIDTH ** -0.5),
        'w_out': nrm(ks[15], (DEPTH, D_MODEL, D_MODEL), D_MODEL ** -0.5),
        'norm_ffn': gain(ks[16], (DEPTH, D_MODEL)),
        'w_gate': nrm(ks[17], (DEPTH, D_MODEL, D_FF), D_MODEL ** -0.5),
        'w_up': nrm(ks[18], (DEPTH, D_MODEL, D_FF), D_MODEL ** -0.5),
        'w_down': nrm(ks[19], (DEPTH, D_FF, D_MODEL), D_FF ** -0.5),
        'norm_final': gain(ks[20], (D_MODEL,)),
    }


def reference(x_prompt, x_sample, norm_mix, w_in, da_lambda_q1, da_lambda_k1, da_lambda_q2,
              da_lambda_k2, da_subln, mla_q_norm, w_uq, mla_kv_norm, w_ukv, w_proj_a, w_proj_b,
              w_out, norm_ffn, w_gate, w_up, w_down, norm_final):
    params = (norm_mix, w_in, da_lambda_q1, da_lambda_k1, da_lambda_q2, da_lambda_k2, da_subln,
              mla_q_norm, w_uq, mla_kv_norm, w_ukv, w_proj_a, w_proj_b, w_out,
              norm_ffn, w_gate, w_up, w_down, norm_final)
    y_prompt = _encoder(x_prompt, params)
    y_sample = _encoder(x_sample, params)
    return (y_prompt, y_sample)
```

```python
import math
from contextlib import ExitStack

import numpy as np
import ml_dtypes

import concourse.bass as bass
import concourse.mybir as mybir
from concourse.bass_utils import run_bass_kernel_spmd

F32 = mybir.dt.float32
BF16 = mybir.dt.bfloat16
AF = mybir.ActivationFunctionType
ALU = mybir.AluOpType

D = 2048
NCH = D // 128
H = 8
DFF = 5632
NFJ = DFF // 512
INW = 8000
QG = 512
EPS = 1e-6
LAMBDA_INIT = 0.8 - 0.6 * math.exp(0.0)
DA_SCALE = 64 ** -0.5
MLA_SCALE = 192 ** -0.5
O_Q, O_K, O_V, O_CQ, O_CKV, O_KR, O_GA, O_GB = 0, 1024, 2048, 3072, 3584, 3840, 3904, 5952


class Buf:
    __slots__ = ("name", "writer", "readers")

    def __init__(self, name):
        self.name = name
        self.writer = None
        self.readers = {}


class Inst:
    __slots__ = ("eng", "fn", "pos", "is_dma", "chan", "count", "marked", "waits")

    def __init__(self, eng, fn, pos, is_dma=False, chan=None):
        self.eng = eng
        self.fn = fn
        self.pos = pos
        self.is_dma = is_dma
        self.chan = chan
        self.count = None
        self.marked = False
        self.waits = []


class Prog:
    ENGS = ("pe", "act", "dve", "pool", "sp")

    def __init__(self, nc):
        self.nc = nc
        self.streams = {e: [] for e in self.ENGS}
        self.waited = {}
        self.chan_last = {}
        self.chan_n = {}
        self.chans = []
        self.ndma = 0

    def _dep(self, I, P):
        if P is None or P is I:
            return
        if P.is_dma:
            key = (I.eng, "c", P.chan)
            if self.waited.get(key, 0) >= P.count:
                return
            self.waited[key] = P.count
            I.waits.append(P)
            return
        if P.eng == "pe" and I.eng == "pe" and not I.is_dma:
            return
        key = (I.eng, "e", P.eng)
        if self.waited.get(key, -1) >= P.pos:
            return
        self.waited[key] = P.pos
        P.marked = True
        I.waits.append(P)

    def _track(self, I, reads, writes):
        for b in reads:
            self._dep(I, b.writer)
        for b in writes:
            self._dep(I, b.writer)
            for R in b.readers.values():
                self._dep(I, R)
        for b in reads:
            if I.is_dma:
                b.readers[("dma", id(I))] = I
            else:
                b.readers[I.eng] = I
        for b in writes:
            b.writer = I
            b.readers = {}

    def op(self, eng, fn, reads=(), writes=()):
        st = self.streams[eng]
        I = Inst(eng, fn, len(st))
        st.append(I)
        self._track(I, reads, writes)
        return I

    def dma(self, eng, out, in_, reads=(), writes=(), chan=None, **kw):
        st = self.streams[eng]
        if chan is None:
            chan = "auto%d" % (self.ndma % 8)
        self.ndma += 1
        if chan not in self.chan_n:
            self.chan_n[chan] = 0
            self.chans.append(chan)
        I = Inst(eng, (lambda e, o=out, i=in_, k=kw: e.dma_start(out=o, in_=i, **k)), len(st), True, chan)
        st.append(I)
        self._dep(I, self.chan_last.get(chan))
        self.chan_n[chan] += 1
        I.count = 16 * self.chan_n[chan]
        self.chan_last[chan] = I
        self._track(I, reads, writes)
        return I

    def emit(self):
        nc = self.nc
        for chan in self.chans:
            I = Inst("sp", None, len(self.streams["sp"]))
            self.streams["sp"].append(I)
            self._dep(I, self.chan_last[chan])
        for e in ("pe", "act", "dve", "pool"):
            c = 0
            for I in self.streams[e]:
                if I.marked and not I.is_dma:
                    c += 1
                    I.count = c
        with ExitStack() as es:
            esem = {e: es.enter_context(nc.semaphore("s_" + e)) for e in ("pe", "act", "dve", "pool")}
            csem = {c: es.enter_context(nc.semaphore("c_%d" % i)) for i, c in enumerate(self.chans)}
            block = es.enter_context(nc.Block())

            def run(stream):
                def f(eng):
                    for I in stream:
                        for P in I.waits:
                            if P.is_dma:
                                eng.wait_ge(csem[P.chan], P.count)
                            else:
                                eng.wait_ge(esem[P.eng], P.count)
                        if I.fn is None:
                            continue
                        bi = I.fn(eng)
                        if I.is_dma:
                            bi.then_inc(csem[I.chan], 16)
                        elif I.marked:
                            bi.then_inc(esem[I.eng], 1)
                return f

            block.tensor(run(self.streams["pe"]))
            block.scalar(run(self.streams["act"]))
            block.vector(run(self.streams["dve"]))
            block.gpsimd(run(self.streams["pool"]))
            block.sync(run(self.streams["sp"]))


class Cfg:
    def __init__(self, S=16384, NQA=8, NQB=4, debug=False, main=True, debug2=False, stop=99):
        self.S = S
        self.NQA = NQA
        self.NQB = NQB
        self.NT = S // QG
        self.NKB = S // 128
        self.KCH = 2048 if S >= 2048 else S
        self.debug = debug
        self.main = main
        self.stop = stop
        self.debug2 = debug2


def build_nc(cfg):
    nc = bass.Bass("TRN2", target_bir_lowering=False)
    P = Prog(nc)
    S, NT, NKB = cfg.S, cfg.NT, cfg.NKB
    NQ = cfg.NQA + cfg.NQB
    slots = [("A", cfg.NQA), ("B", cfg.NQB)]

    def din(name, shape, dt=F32):
        return nc.dram_tensor(name, list(shape), dt, kind="ExternalInput").ap()

    def dout(name, shape, dt=F32):
        return nc.dram_tensor(name, list(shape), dt, kind="ExternalOutput").ap()

    def dscr(name, shape, dt=BF16):
        return nc.dram_tensor(name, list(shape), dt).ap()

    xkv = {"A": din("xkvA", [S, D]), "B": din("xkvB", [S, D])}
    rope = {"A": din("ropeA", [2, 64, S]), "B": din("ropeB", [2, 64, S])}
    alibi = din("alibi", [NQ, 2, NKB * H])
    cst = din("cst", [128, 128 + 512 * 5])
    w_in = din("w_in", [D, INW])
    w_uq = din("w_uq", [512, 1536])
    w_ukv = din("w_ukv", [256, 2048])
    w_pa = din("w_proj_a", [1024, D])
    w_pb = din("w_proj_b", [1024, D])
    w_out = din("w_out", [D, D])
    w_gate = din("w_gate", [D, DFF])
    w_up = din("w_up", [D, DFF])
    w_down = din("w_down", [DFF, D])
    vec = {n: din(n, [1, sz]) for n, sz in (("norm_mix", D), ("norm_ffn", D), ("norm_final", D),
                                            ("da_subln", 128), ("mla_q_norm", 512), ("mla_kv_norm", 256),
                                            ("lq1", 64), ("lk1", 64), ("lq2", 64), ("lk2", 64))}
    yout = {"A": dout("yA", [cfg.NQA * QG, D]), "B": dout("yB", [cfg.NQB * QG, D])}

    wb_in = dscr("wb_in", [D, INW])
    wb_uqx = dscr("wb_uqx", [512, 2048])
    wb_pa = dscr("wb_pa", [1024, D])
    wb_pb = dscr("wb_pb", [1024, D])
    wb_out = dscr("wb_out", [D, D])
    wb_gate = dscr("wb_gate", [D, DFF])
    wb_up = dscr("wb_up", [D, DFF])
    wb_down = dscr("wb_down", [DFF, D])
    kv = {}
    for s, _ in slots:
        kv[s] = dict(
            kda=dscr("kda" + s, [H, 128, S]),
            vda=dscr("vda" + s, [128, H, NKB, 128]),
            kn=dscr("kn" + s, [H, 128, S]),
            kr=dscr("kr" + s, [64, S]),
            vb=dscr("vb" + s, [128, H, NKB, 128]),
        )
    b_wb = Buf("wb")
    b_kv = {s: Buf("kv" + s) for s, _ in slots}

    dbg = {}
    if cfg.debug2:
        dbg["RQ"] = dout("dbg_RQ", [128, 24 * QG], BF16)
        dbg["RH"] = dout("dbg_RH", [128, NCH * QG], BF16)
        dbg["x1"] = dout("dbg_x1", [128, NCH * QG], F32)
        dbg["RQ2"] = dout("dbg_RQ2", [128, 24 * QG], BF16)
        for jj in range(4):
            dbg["xj%d" % jj] = dout("dbg_xj%d" % jj, [128, NCH * QG], F32)
            dbg["mj%d" % jj] = dout("dbg_mj%d" % jj, [128, 24 * QG], BF16)
        dbg["x2"] = dout("dbg_x2", [128, NCH * QG], F32)
    if cfg.debug:
        dbg["kda"] = dout("dbg_kda", [H, 128, S], BF16)
        dbg["vda"] = dout("dbg_vda", [128, H, NKB, 128], BF16)
        dbg["kn"] = dout("dbg_kn", [H, 128, S], BF16)
        dbg["kr"] = dout("dbg_kr", [64, S], BF16)
        dbg["vb"] = dout("dbg_vb", [128, H, NKB, 128], BF16)

    U8 = mybir.dt.uint8
    ARENA = 206 * 1024
    arena = nc.alloc_sbuf_tensor("arena", [128, ARENA], U8).ap()
    DTSZ = {F32: 4, BF16: 2}

    class Arena:
        def __init__(self, base, limit):
            self.off = base
            self.limit = limit

        def alloc(self, shape, dt, parts=128):
            n = int(np.prod(shape)) * DTSZ[dt]
            off = self.off
            self.off += (n + 63) // 64 * 64
            assert self.off <= self.limit, (self.off, self.limit)
            v = arena[0:parts, off:off + n].bitcast(dt)
            if len(shape) == 2:
                v = v.rearrange("p (a b) -> p a b", a=shape[0])
            elif len(shape) == 3:
                v = v.rearrange("p (a b c) -> p a b c", a=shape[0], b=shape[1])
            return v

    SH = Arena(0, ARENA)
    psum = [nc.alloc_psum_tensor("ps%d" % i, [128, 512], F32).ap() for i in range(8)]
    b_ps = [Buf("ps%d" % i) for i in range(8)]

    ident = SH.alloc([128], F32)
    ones_f = SH.alloc([128], F32)
    ones_b = SH.alloc([128], BF16)
    b_const = Buf("const")
    gcol = {}
    for n, sz in (("norm_mix", D), ("norm_ffn", D), ("norm_final", D), ("da_subln", 128),
                  ("mla_q_norm", 512), ("mla_kv_norm", 256)):
        gcol[n] = SH.alloc([sz // 128], F32)
    gsub08 = SH.alloc([1], F32)
    lam_t = SH.alloc([4, 64], F32)
    lam_s = SH.alloc([4], F32)
    neglam = SH.alloc([1], F32)
    xin = [SH.alloc([D], F32) for i in range(2)]
    b_xin = [Buf("xin%d" % i) for i in range(2)]
    xT = SH.alloc([NCH, QG], F32)
    b_xT = [Buf("xT%d" % c) for c in range(NCH)]
    RH = SH.alloc([NCH, QG], BF16)
    b_RH = [Buf("RH%d" % c) for c in range(NCH)]
    sq = [SH.alloc([QG], F32) for i in range(2)]
    b_sq = [Buf("sq%d" % i) for i in range(2)]
    rstd = SH.alloc([QG], F32)
    b_rstd = Buf("rstd")
    PH_BASE = SH.off

    P.dma("sp", ident, cst[:, 0:128], writes=[b_const], chan="cst")
    for n in gcol:
        P.dma("sp", gcol[n], vec[n].rearrange("o (c p) -> p (o c)", p=128), writes=[b_const], chan="cst",
              allow_slow_non_contiguous=True)
    for i, n in enumerate(("lq1", "lk1", "lq2", "lk2")):
        P.dma("sp", lam_t[:, i, :], vec[n].partition_broadcast(128), writes=[b_const], chan="cst")
    P.op("pool", lambda e: e.memset(ones_f, 1.0), writes=[b_const])
    P.op("pool", lambda e: e.memset(ones_b, 1.0), writes=[b_const])
    b_lam = Buf("lam")
    P.op("dve", lambda e: e.tensor_scalar(out=gsub08, in0=gcol["da_subln"], scalar1=1.0 - LAMBDA_INIT, scalar2=None,
                                          op0=ALU.mult), reads=[b_const], writes=[b_lam])
    P.op("dve", lambda e: e.tensor_tensor(out=lam_t[:, 0, :], in0=lam_t[:, 0, :], in1=lam_t[:, 1, :], op=ALU.mult),
         reads=[b_const, b_lam], writes=[b_lam])
    P.op("dve", lambda e: e.tensor_tensor(out=lam_t[:, 2, :], in0=lam_t[:, 2, :], in1=lam_t[:, 3, :], op=ALU.mult),
         reads=[b_const, b_lam], writes=[b_lam])
    P.op("dve", lambda e: e.reduce_sum(out=lam_s[:, 0:1], in_=lam_t[:, 0, :], axis=mybir.AxisListType.X),
         reads=[b_lam], writes=[b_lam])
    P.op("dve", lambda e: e.reduce_sum(out=lam_s[:, 1:2], in_=lam_t[:, 2, :], axis=mybir.AxisListType.X),
         reads=[b_lam], writes=[b_lam])
    P.op("act", lambda e: e.activation(out=lam_s[:, 2:4], in_=lam_s[:, 0:2], func=AF.Exp),
         reads=[b_lam], writes=[b_lam])
    P.op("dve", lambda e: e.tensor_tensor(out=neglam, in0=lam_s[:, 3:4], in1=lam_s[:, 2:3], op=ALU.subtract),
         reads=[b_lam], writes=[b_lam])
    P.op("dve", lambda e: e.tensor_scalar(out=neglam, in0=neglam, scalar1=-LAMBDA_INIT, scalar2=None, op0=ALU.add),
         reads=[b_lam], writes=[b_lam])

    cc_n = [0]

    def cast_copy(dst, src, rows):
        RB = 128
        cols = src.shape[1]
        a = 1
        while cols // a > 2048 or cols % a:
            a += 1
        for r0 in range(0, rows, RB):
            r1 = min(rows, r0 + RB)
            P.dma("pool", dst[r0:r1, :].rearrange("r (a b) -> r a b", a=a),
                  src[r0:r1, :].rearrange("r (a b) -> r a b", a=a), writes=[b_wb], chan="wcast%d" % (cc_n[0] % 4))
            cc_n[0] += 1


    ev_cnt = [0]

    def evac(dst, src, reads, writes):
        ev_cnt[0] += 1
        if ev_cnt[0] % 2 == 0:
            P.op("act", lambda e: e.copy(out=dst, in_=src), reads=reads, writes=writes)
        else:
            P.op("dve", lambda e: e.tensor_copy(out=dst, in_=src), reads=reads, writes=writes)

    def load_xT(src_rows):
        for b in range(4):
            bi = b % 2
            P.dma("sp", xin[bi], src_rows[b * 128:(b + 1) * 128, :], writes=[b_xin[bi]], chan="xin%d" % bi)
            for c4 in range(NCH // 4):
                pb = c4 % 2
                for k in range(4):
                    c = c4 * 4 + k
                    P.op("pe", lambda e, c=c, k=k, pb=pb, bi=bi: e.transpose(
                        out=psum[pb][:, k * 128:(k + 1) * 128], in_=xin[bi][:, c * 128:(c + 1) * 128], identity=ident),
                        reads=[b_xin[bi], b_const], writes=[b_ps[pb]])
                evac(xT[:, c4 * 4:(c4 + 1) * 4, b * 128:(b + 1) * 128], psum[pb].rearrange("p (k t) -> p k t", k=4),
                     [b_ps[pb]], b_xT[c4 * 4:(c4 + 1) * 4])

    def norm_stats(src, b_src, nchunks, width, pbank, parts=128):
        for c in range(nchunks):
            si = c % 2
            P.op("act", lambda e, c=c, si=si: e.activation(out=sq[si], in_=src[:, c, :], func=AF.Square),
                 reads=[b_src[c]], writes=[b_sq[si]])
            P.op("pe", lambda e, c=c, si=si: e.matmul(psum[pbank], lhsT=ones_f, rhs=sq[si], start=(c == 0),
                                                      stop=(c == nchunks - 1)),
                 reads=[b_sq[si], b_const], writes=[b_ps[pbank]])
        P.op("act", lambda e: e.activation(out=rstd, in_=psum[pbank], func=AF.Sqrt, scale=1.0 / width, bias=EPS),
             reads=[b_ps[pbank]], writes=[b_rstd])
        P.op("dve", lambda e: e.reciprocal(out=rstd, in_=rstd), reads=[b_rstd], writes=[b_rstd])

    def norm_apply(src, b_src, g, nchunks, dst, b_dst, engs=("dve",)):
        for c in range(nchunks):
            eng = engs[c % len(engs)]
            P.op(eng, lambda e, c=c: e.scalar_tensor_tensor(out=dst[:, c, :], in0=src[:, c, :], scalar=g[:, c:c + 1],
                                                            in1=rstd, op0=ALU.mult, op1=ALU.mult),
                 reads=[b_src[c], b_rstd, b_const], writes=[b_dst[c]])

    def barrier():
        lasts = {e: (P.streams[e][-1] if P.streams[e] else None) for e in ("pe", "act", "dve", "pool")}
        chans = list(P.chans)
        for x in ("pe", "act", "dve", "pool", "sp"):
            I = Inst(x, None, len(P.streams[x]))
            P.streams[x].append(I)
            for e, L in lasts.items():
                if L is not None and e != x:
                    key = (x, "e", e)
                    if P.waited.get(key, -1) < L.pos:
                        P.waited[key] = L.pos
                        L.marked = True
                        I.waits.append(L)
            for c in chans:
                P._dep(I, P.chan_last[c])

    if cfg.main:
        XB = xT.rearrange("p c t -> p (c t)").bitcast(BF16)
        uq_src = XB[:, 0:4 * 1536].rearrange("p (c n) -> p c n", c=4)
        uq_ext = XB[:, 6144:6144 + 4 * 2048].rearrange("p (c n) -> p c n", c=4)
        P.dma("pool", uq_src, w_uq.rearrange("(c p) n -> p c n", p=128), writes=b_xT, chan="wkv0")
        for c in range(4):
            sv = uq_src[:, c, :].rearrange("p (h d) -> p h d", h=H)
            dv = uq_ext[:, c, :].rearrange("p (h d) -> p h d", h=H)
            P.op("dve", lambda e, sv=sv, dv=dv: e.tensor_copy(out=dv[:, :, 0:192], in_=sv[:, :, 0:192]),
                 reads=b_xT, writes=b_xT)
            P.op("dve", lambda e, sv=sv, dv=dv: e.tensor_scalar(out=dv[:, :, 192:224], in0=sv[:, :, 160:192], scalar1=-1.0,
                                                                scalar2=None, op0=ALU.mult), reads=b_xT, writes=b_xT)
            P.op("dve", lambda e, sv=sv, dv=dv: e.tensor_copy(out=dv[:, :, 224:256], in_=sv[:, :, 128:160]),
                 reads=b_xT, writes=b_xT)
        P.dma("pool", wb_uqx.rearrange("(c p) n -> p c n", p=128), uq_ext, reads=b_xT, writes=[b_wb], chan="wkv0")
        cast_copy(wb_in, w_in, D)
        cast_copy(wb_pa, w_pa, 1024)
        cast_copy(wb_pb, w_pb, 1024)
        cast_copy(wb_out, w_out, D)
        cast_copy(wb_gate, w_gate, D)
        cast_copy(wb_up, w_up, D)
        cast_copy(wb_down, w_down, DFF)

    A1 = Arena(PH_BASE, ARENA)
    wk = A1.alloc([NCH, 1024], BF16)
    wv = A1.alloc([NCH, 1024], BF16)
    wc = A1.alloc([NCH, 384], BF16)
    wu = A1.alloc([2, 2048], BF16)
    b_wkv = Buf("wkv")
    w_in_v = w_in.rearrange("(c p) n -> p c n", p=128)
    for c in range(NCH):
        P.dma("pool", wk[:, c, :], w_in_v[:, c, O_K:O_K + 1024], writes=[b_wkv], chan="wkv0")
        P.dma("pool", wv[:, c, :], w_in_v[:, c, O_V:O_V + 1024], writes=[b_wkv], chan="wkv1")
    P.dma("pool", wc[:, :, 0:320], w_in_v[:, :, O_CKV:O_CKV + 320], writes=[b_wkv], chan="wkv2")
    P.dma("pool", wc[:, :, 320:352], w_in_v[:, :, O_KR + 32:O_KR + 64], writes=[b_wkv], chan="wkv2")
    P.dma("pool", wc[:, :, 352:384], w_in_v[:, :, O_KR:O_KR + 32], writes=[b_wkv], chan="wkv2")
    P.op("dve", lambda e: e.tensor_scalar(out=wc[:, :, 320:352], in0=wc[:, :, 320:352], scalar1=-1.0, scalar2=None,
                                          op0=ALU.mult), reads=[b_wkv], writes=[b_wkv])
    w_ukv_v = w_ukv.rearrange("(c p) (h t d) -> p c h t d", p=128, h=H, t=2)
    for c in range(2):
        for t in range(2):
            P.dma("pool", wu[:, c, t * 1024:(t + 1) * 1024].rearrange("p (h d) -> p h d", h=H),
                  w_ukv_v[:, c, :, t, :], writes=[b_wkv], chan="wkv3")

    ckvT = A1.alloc([2, QG], F32)
    b_ckvT = [Buf("ckvT0"), Buf("ckvT1")]
    ckvn = A1.alloc([2, QG], BF16)
    b_ckvn = [Buf("ckvn0"), Buf("ckvn1")]
    krT = A1.alloc([2, QG], F32, parts=64)
    b_krT = [Buf("krT0"), Buf("krT1")]
    ropet = A1.alloc([2, QG], F32, parts=64)
    b_ropet = Buf("ropet")
    krtmp = A1.alloc([QG], F32, parts=64)
    b_krtmp = Buf("krtmp")
    NST = 3
    stK = [A1.alloc([QG], BF16) for i in range(NST)]
    b_stK = [Buf("stK%d" % i) for i in range(NST)]
    stV = [A1.alloc([H, 4, 128], BF16) for i in range(2)]
    b_stV = [[Buf("stV%d_%d" % (i, j)) for j in range(8)] for i in range(2)]
    stR = A1.alloc([QG], BF16, parts=64)
    b_stR = Buf("stR")
    g_kv = gcol["mla_kv_norm"]
    hT = RH
    b_hT = b_RH
    stk_i = [0]

    def kv_tile(s, t):
        sc = kv[s]
        load_xT(xkv[s][t * QG:(t + 1) * QG, :])
        P.dma("sp", ropet, rope[s][:, :, t * QG:(t + 1) * QG].rearrange("a p t -> p a t"), writes=[b_ropet],
              chan="ropet")
        norm_stats(xT, b_xT, NCH, D, 2)
        norm_apply(xT, b_xT, gcol["norm_mix"], NCH, hT, b_hT)
        for h in range(H):
            pb = 3 + (h % 2)
            for c in range(NCH):
                P.op("pe", lambda e, c=c, h=h, pb=pb: e.matmul(psum[pb], lhsT=wk[:, c, h * 128:(h + 1) * 128],
                                                              rhs=hT[:, c, :], start=(c == 0), stop=(c == NCH - 1)),
                     reads=[b_wkv, b_hT[c]], writes=[b_ps[pb]])
            si = stk_i[0] % NST
            stk_i[0] += 1
            evac(stK[si], psum[pb], [b_ps[pb]], [b_stK[si]])
            P.dma("pool", sc["kda"][h, :, t * QG:(t + 1) * QG], stK[si], reads=[b_stK[si]], writes=[b_kv[s]],
                  chan="stK%d" % si)
        vi = 0
        for blk in range(4):
            for half in range(2):
                pb = 5 + half
                for c in range(NCH):
                    P.op("pe", lambda e, c=c, blk=blk, half=half, pb=pb: e.matmul(
                        psum[pb], lhsT=hT[:, c, blk * 128:(blk + 1) * 128], rhs=wv[:, c, half * 512:(half + 1) * 512],
                        start=(c == 0), stop=(c == NCH - 1)), reads=[b_wkv, b_hT[c]], writes=[b_ps[pb]])
                evac(stV[vi][:, half * 4:(half + 1) * 4, blk, :], psum[pb].rearrange("p (h d) -> p h d", h=4),
                     [b_ps[pb]], [b_stV[vi][blk * 2 + half]])
        P.dma("pool", sc["vda"][:, :, t * 4:(t + 1) * 4, :], stV[vi], reads=b_stV[vi], writes=[b_kv[s]],
              chan="stV%d" % vi)
        for j in range(2):
            pb = 3 + j
            for c in range(NCH):
                P.op("pe", lambda e, c=c, j=j, pb=pb: e.matmul(psum[pb], lhsT=wc[:, c, j * 128:(j + 1) * 128],
                                                              rhs=hT[:, c, :], start=(c == 0), stop=(c == NCH - 1)),
                     reads=[b_wkv, b_hT[c]], writes=[b_ps[pb]])
            evac(ckvT[:, j, :], psum[pb], [b_ps[pb]], [b_ckvT[j]])
        for j in range(2):
            pb = 5 + j
            for c in range(NCH):
                P.op("pe", lambda e, c=c, j=j, pb=pb: e.matmul(psum[pb][0:64, :], lhsT=wc[:, c, 256 + j * 64:320 + j * 64],
                                                              rhs=hT[:, c, :], start=(c == 0), stop=(c == NCH - 1)),
                     reads=[b_wkv, b_hT[c]], writes=[b_ps[pb]])
            evac(krT[:, j, :], psum[pb][0:64, :], [b_ps[pb]], [b_krT[j]])
        P.op("dve", lambda e: e.tensor_tensor(out=krtmp, in0=krT[:, 0, :], in1=ropet[:, 0, :], op=ALU.mult),
             reads=[b_krT[0], b_ropet], writes=[b_krtmp])
        P.op("dve", lambda e: e.tensor_tensor(out=krT[:, 1, :], in0=krT[:, 1, :], in1=ropet[:, 1, :], op=ALU.mult),
             reads=[b_krT[1], b_ropet], writes=[b_krT[1]])
        P.op("dve", lambda e: e.tensor_tensor(out=stR, in0=krtmp, in1=krT[:, 1, :], op=ALU.add),
             reads=[b_krT[1], b_krtmp], writes=[b_stR])
        P.dma("pool", sc["kr"][:, t * QG:(t + 1) * QG], stR, reads=[b_stR], writes=[b_kv[s]], chan="stR")
        norm_stats(ckvT, b_ckvT, 2, 256, 2)
        norm_apply(ckvT, b_ckvT, g_kv, 2, ckvn, b_ckvn)
        for h in range(H):
            pb = 3 + (h % 2)
            for c in range(2):
                P.op("pe", lambda e, c=c, h=h, pb=pb: e.matmul(psum[pb], lhsT=wu[:, c, h * 128:(h + 1) * 128],
                                                              rhs=ckvn[:, c, :], start=(c == 0), stop=(c == 1)),
                     reads=[b_wkv, b_ckvn[c]], writes=[b_ps[pb]])
            si = stk_i[0] % NST
            stk_i[0] += 1
            evac(stK[si], psum[pb], [b_ps[pb]], [b_stK[si]])
            P.dma("pool", sc["kn"][h, :, t * QG:(t + 1) * QG], stK[si], reads=[b_stK[si]], writes=[b_kv[s]],
                  chan="stK%d" % si)
        vi2 = 1
        for blk in range(4):
            for half in range(2):
                pb = 5 + half
                for c in range(2):
                    P.op("pe", lambda e, c=c, blk=blk, half=half, pb=pb: e.matmul(
                        psum[pb], lhsT=ckvn[:, c, blk * 128:(blk + 1) * 128],
                        rhs=wu[:, c, 1024 + half * 512:1024 + (half + 1) * 512], start=(c == 0), stop=(c == 1)),
                        reads=[b_wkv, b_ckvn[c]], writes=[b_ps[pb]])
                evac(stV[vi2][:, half * 4:(half + 1) * 4, blk, :], psum[pb].rearrange("p (h d) -> p h d", h=4),
                     [b_ps[pb]], [b_stV[vi2][blk * 2 + half]])
        P.dma("pool", sc["vb"][:, :, t * 4:(t + 1) * 4, :], stV[vi2], reads=b_stV[vi2], writes=[b_kv[s]],
              chan="stV%d" % vi2)

    for s, _ in slots:
        for t in range(NT):
            kv_tile(s, t)

    if cfg.debug:
        for k in ("kda", "vda", "kn", "kr", "vb"):
            P.dma("sp", dbg[k], kv["A"][k], reads=[b_kv["A"]], chan="dbg")

    if not cfg.main:
        P.emit()
        return nc

    barrier()
    if cfg.stop == 0:
        P.emit()
        return nc
    A2 = Arena(PH_BASE, ARENA)
    NSLOT = 3
    ring = [A2.alloc([8192], BF16) for _ in range(NSLOT)]
    b_ring = [Buf("ring%d" % k) for k in range(NSLOT)]
    RQ = A2.alloc([24, QG], BF16)
    b_RQ = [Buf("RQ%d" % c) for c in range(24)]
    NPT = 6
    Pt = [A2.alloc([QG], BF16) for _ in range(NPT)]
    b_Pt = [Buf("Pt%d" % k) for k in range(NPT)]
    tmpS = [A2.alloc([QG], F32) for _ in range(4)]
    b_tmpS = [Buf("tmpS%d" % k) for k in range(4)]
    Dt = A2.alloc([5, QG], F32)
    b_Dt = Buf("Dt")
    c1tab = A2.alloc([NKB * H], F32)
    btab = A2.alloc([NKB * H], F32)
    b_tab = Buf("tab")
    ropeq = A2.alloc([2, QG], F32, parts=64)
    b_ropeq = Buf("ropeq")
    scr = [A2.alloc([QG], F32) for _ in range(6)]
    b_scr = [Buf("scr%d" % k) for k in range(6)]
    cqT = xin[0].rearrange("p (c t) -> p c t", c=4)
    cqn = xin[1][:, 0:1024].bitcast(BF16).rearrange("p (c t) -> p c t", c=4)
    qrt = xin[1][0:64, 1024:2048].rearrange("p (c t) -> p c t", c=2)
    KCH = cfg.KCH
    KB = KCH // 128
    NCHK = S // KCH

    P.dma("sp", Dt, cst[:, 128:128 + 5 * QG].rearrange("p (a t) -> p a t", a=5), writes=[b_Dt], chan="cst")

    ring_n = [0]

    def ring_next():
        k = ring_n[0] % NSLOT
        ring_n[0] += 1
        return ring[k], b_ring[k], "ring%d" % k

    bank_n = [0]

    def nb():
        bank_n[0] += 1
        return bank_n[0] % 8

    def wtile(src2d, kc, ncols):
        slot, bf, ch = ring_next()
        view = slot[:, 0:kc * ncols].rearrange("p (c n) -> p c n", c=kc)
        P.dma("sp", view, src2d.rearrange("(c p) n -> p c n", p=128), reads=[b_wb], writes=[bf], chan=ch)
        return view, bf

    def da_epilogue(h):
        P.op("dve", lambda e: e.reciprocal(out=scr[0], in_=psum[2]), reads=[b_ps[2]], writes=[b_scr[0]])
        P.op("dve", lambda e: e.tensor_tensor(out=scr[1], in0=psum[0], in1=scr[0], op=ALU.mult),
             reads=[b_ps[0], b_scr[0]], writes=[b_scr[1]])
        P.op("dve", lambda e: e.reciprocal(out=scr[2], in_=psum[3]), reads=[b_ps[3]], writes=[b_scr[2]])
        P.op("dve", lambda e: e.tensor_tensor(out=scr[3], in0=psum[1], in1=scr[2], op=ALU.mult),
             reads=[b_ps[1], b_scr[2]], writes=[b_scr[3]])
        P.op("dve", lambda e: e.scalar_tensor_tensor(out=scr[4], in0=scr[3], scalar=neglam, in1=scr[1],
                                                     op0=ALU.mult, op1=ALU.add),
             reads=[b_scr[3], b_scr[1], b_lam], writes=[b_scr[4]])
        P.op("act", lambda e: e.activation(out=sq[0], in_=scr[4], func=AF.Square), reads=[b_scr[4]], writes=[b_sq[0]])
        P.op("pe", lambda e: e.matmul(psum[0], lhsT=ones_f, rhs=sq[0], start=True, stop=True),
             reads=[b_sq[0], b_const], writes=[b_ps[0]])
        P.op("act", lambda e: e.activation(out=rstd, in_=psum[0], func=AF.Sqrt, scale=1.0 / 128, bias=EPS),
             reads=[b_ps[0]], writes=[b_rstd])
        P.op("dve", lambda e: e.reciprocal(out=rstd, in_=rstd), reads=[b_rstd], writes=[b_rstd])
        P.op("dve", lambda e: e.scalar_tensor_tensor(out=RH[:, h, :], in0=scr[4], scalar=gsub08, in1=rstd,
                                                     op0=ALU.mult, op1=ALU.mult),
             reads=[b_scr[4], b_rstd, b_lam], writes=[b_RH[h]])

    def attn_da(s, i, h):
        steps = [(c, kb) for c in range(NCHK) for kb in range(KB)]
        slot_of = {}
        sbanks = [(4, 5), (6, 7)]

        def issue_S(n):
            c, kb = steps[n]
            if kb == 0:
                slot, bf, ch = ring_next()
                P.dma("sp", slot[:, 0:KCH], kv[s]["kda"][h, :, c * KCH:(c + 1) * KCH], reads=[b_kv[s]], writes=[bf],
                      chan=ch)
                P.dma("sp", slot[:, KCH:2 * KCH].rearrange("p (b d) -> p b d", d=128),
                      kv[s]["vda"][:, h, c * KB:(c + 1) * KB, :], reads=[b_kv[s]], writes=[bf], chan=ch)
                slot_of[c] = (slot, bf)
            slot, bf = slot_of[c]
            r = c * KB + kb
            col = r * H + h
            din = Dt[:, 1 + (r - 4 * i), :] if 4 * i <= r < 4 * i + 4 else Dt[:, 0, :]
            for m in range(2):
                pb = sbanks[n % 2][m]
                ti = 2 * (n % 2) + m
                P.op("pe", lambda e, m=m, pb=pb, slot=slot, kb=kb: e.matmul(
                    psum[pb], lhsT=slot[m * 64:(m + 1) * 64, kb * 128:(kb + 1) * 128],
                    rhs=RQ[m * 64:(m + 1) * 64, h, :], start=True, stop=True),
                    reads=[bf, b_RQ[h]], writes=[b_ps[pb]])
                P.op("dve", lambda e, pb=pb, ti=ti, din=din, col=col: e.scalar_tensor_tensor(
                    out=tmpS[ti], in0=din, scalar=c1tab[:, col:col + 1], in1=psum[pb], op0=ALU.mult, op1=ALU.add),
                    reads=[b_ps[pb], b_Dt, b_tab], writes=[b_tmpS[ti]])
                P.op("act", lambda e, ti=ti, col=col: e.activation(out=Pt[ti], in_=tmpS[ti], func=AF.Exp,
                                                                 scale=DA_SCALE, bias=btab[:, col:col + 1]),
                     reads=[b_tmpS[ti], b_tab], writes=[b_Pt[ti]])

        def issue_PV(n):
            c, kb = steps[n]
            slot, bf = slot_of[c]
            V = slot[:, KCH + kb * 128:KCH + (kb + 1) * 128]
            first, last = (n == 0), (n == len(steps) - 1)
            for m in range(2):
                ti = 2 * (n % 2) + m
                P.op("pe", lambda e, m=m, ti=ti, V=V: e.matmul(psum[m], lhsT=V, rhs=Pt[ti], start=first, stop=last),
                     reads=[bf, b_Pt[ti]], writes=[b_ps[m]])
                P.op("pe", lambda e, m=m, ti=ti: e.matmul(psum[2 + m], lhsT=ones_b, rhs=Pt[ti], start=first, stop=last),
                     reads=[b_const, b_Pt[ti]], writes=[b_ps[2 + m]])

        issue_S(0)
        for n in range(len(steps)):
            if n + 1 < len(steps):
                issue_S(n + 1)
            issue_PV(n)

    def attn_mla(s, i, h, hook):
        steps = [(c, kb) for c in range(NCHK) for kb in range(KB)]
        slot_of = {}

        def issue_S(n):
            c, kb = steps[n]
            if kb == 0:
                slot, bf, ch = ring_next()
                P.dma("sp", slot[:, 0:KCH], kv[s]["kn"][h, :, c * KCH:(c + 1) * KCH], reads=[b_kv[s]], writes=[bf],
                      chan=ch)
                P.dma("sp", slot[0:64, KCH:2 * KCH], kv[s]["kr"][:, c * KCH:(c + 1) * KCH], reads=[b_kv[s]],
                      writes=[bf], chan=ch)
                P.dma("sp", slot[:, 2 * KCH:3 * KCH].rearrange("p (b d) -> p b d", d=128),
                      kv[s]["vb"][:, h, c * KB:(c + 1) * KB, :], reads=[b_kv[s]], writes=[bf], chan=ch)
                slot_of[c] = (slot, bf)
            slot, bf = slot_of[c]
            pb = 4 + (n % 2)
            ti = 4 + (n % 2)
            P.op("pe", lambda e, pb=pb, slot=slot, kb=kb: e.matmul(
                psum[pb], lhsT=slot[:, kb * 128:(kb + 1) * 128], rhs=RQ[:, 8 + h, :], start=True, stop=False),
                reads=[bf, b_RQ[8 + h]], writes=[b_ps[pb]])
            P.op("pe", lambda e, pb=pb, slot=slot, kb=kb: e.matmul(
                psum[pb], lhsT=slot[0:64, KCH + kb * 128:KCH + (kb + 1) * 128], rhs=RQ[0:64, 16 + h, :],
                start=False, stop=True), reads=[bf, b_RQ[16 + h]], writes=[b_ps[pb]])
            P.op("act", lambda e, pb=pb, ti=ti: e.activation(out=Pt[ti], in_=psum[pb], func=AF.Exp, scale=MLA_SCALE),
                 reads=[b_ps[pb]], writes=[b_Pt[ti]])

        def issue_PV(n):
            c, kb = steps[n]
            slot, bf = slot_of[c]
            V = slot[:, 2 * KCH + kb * 128:2 * KCH + (kb + 1) * 128]
            first, last = (n == 0), (n == len(steps) - 1)
            ti = 4 + (n % 2)
            P.op("pe", lambda e, ti=ti, V=V: e.matmul(psum[6], lhsT=V, rhs=Pt[ti], start=first, stop=last),
                 reads=[bf, b_Pt[ti]], writes=[b_ps[6]])
            P.op("pe", lambda e, ti=ti: e.matmul(psum[7], lhsT=ones_b, rhs=Pt[ti], start=first, stop=last),
                 reads=[b_const, b_Pt[ti]], writes=[b_ps[7]])

        issue_S(0)
        for n in range(len(steps)):
            if n + 1 < len(steps):
                issue_S(n + 1)
            issue_PV(n)
            if n == min(2, len(steps) - 1):
                hook()
        P.op("dve", lambda e: e.reciprocal(out=scr[5], in_=psum[7]), reads=[b_ps[7]], writes=[b_scr[5]])
        P.op("dve", lambda e: e.tensor_tensor(out=RH[:, 8 + h, :], in0=psum[6], in1=scr[5], op=ALU.mult),
             reads=[b_ps[6], b_scr[5]], writes=[b_RH[8 + h]])

    def qgroup(s, i, q):
        rows = xkv[s][i * QG:(i + 1) * QG, :]
        load_xT(rows)
        P.dma("sp", c1tab, alibi[q, 0:1, :].partition_broadcast(128), writes=[b_tab], chan="tab")
        P.dma("sp", btab, alibi[q, 1:2, :].partition_broadcast(128), writes=[b_tab], chan="tab")
        P.dma("sp", ropeq, rope[s][:, :, i * QG:(i + 1) * QG].rearrange("a p t -> p a t"), writes=[b_ropeq],
              chan="ropeq")
        norm_stats(xT, b_xT, NCH, D, nb())
        norm_apply(xT, b_xT, gcol["norm_mix"], NCH, RH, b_RH)
        for tl in range(2):
            view, bf = wtile(wb_in[:, O_Q + tl * 512:O_Q + (tl + 1) * 512], NCH, 512)
            for cc in range(4):
                pb = nb()
                h = tl * 4 + cc
                for c in range(NCH):
                    P.op("pe", lambda e, c=c, cc=cc, pb=pb, view=view: e.matmul(
                        psum[pb], lhsT=view[:, c, cc * 128:(cc + 1) * 128], rhs=RH[:, c, :], start=(c == 0),
                        stop=(c == NCH - 1)), reads=[bf, b_RH[c]], writes=[b_ps[pb]])
                evac(RQ[:, h, :], psum[pb], [b_ps[pb]], [b_RQ[h]])
        view, bf = wtile(wb_in[:, O_CQ:O_CQ + 512], NCH, 512)
        for cc in range(4):
            pb = nb()
            for c in range(NCH):
                P.op("pe", lambda e, c=c, cc=cc, pb=pb, view=view: e.matmul(
                    psum[pb], lhsT=view[:, c, cc * 128:(cc + 1) * 128], rhs=RH[:, c, :], start=(c == 0),
                    stop=(c == NCH - 1)), reads=[bf, b_RH[c]], writes=[b_ps[pb]])
            evac(cqT[:, cc, :], psum[pb], [b_ps[pb]], [b_xin[0]])
        norm_stats(cqT, [b_xin[0]] * 4, 4, 512, nb())
        norm_apply(cqT, [b_xin[0]] * 4, gcol["mla_q_norm"], 4, cqn, [b_xin[1]] * 4, engs=("dve",))
        view, bf = wtile(wb_uqx, 4, 2048)
        for h in range(H):
            pb = nb()
            for c in range(4):
                P.op("pe", lambda e, c=c, h=h, pb=pb, view=view: e.matmul(
                    psum[pb], lhsT=view[:, c, h * 256:h * 256 + 128], rhs=cqn[:, c, :], start=(c == 0), stop=(c == 3)),
                    reads=[bf, b_xin[1]], writes=[b_ps[pb]])
            evac(RQ[:, 8 + h, :], psum[pb], [b_ps[pb]], [b_RQ[8 + h]])
            pbs = (nb(), nb())
            for j in range(2):
                for c in range(4):
                    P.op("pe", lambda e, c=c, h=h, j=j, pb=pbs[j], view=view: e.matmul(
                        psum[pb][0:64, :], lhsT=view[:, c, h * 256 + 128 + j * 64:h * 256 + 192 + j * 64],
                        rhs=cqn[:, c, :], start=(c == 0), stop=(c == 3)), reads=[bf, b_xin[1]], writes=[b_ps[pbs[j]]])
                P.op("dve", lambda e, j=j, pb=pbs[j]: e.tensor_tensor(out=qrt[:, j, :], in0=psum[pb][0:64, :],
                                                                     in1=ropeq[:, j, :], op=ALU.mult),
                     reads=[b_ps[pbs[j]], b_ropeq], writes=[b_xin[1]])
            P.op("dve", lambda e, h=h: e.tensor_tensor(out=RQ[0:64, 16 + h, :], in0=qrt[:, 0, :], in1=qrt[:, 1, :],
                                                       op=ALU.add), reads=[b_xin[1]], writes=[b_RQ[16 + h]])
        if cfg.debug2 and q == 0:
            P.dma("pool", dbg["RQ"], RQ.rearrange("p c t -> p (c t)"), reads=b_RQ, chan="dbg")
        if cfg.stop == 1:
            return
        pending = [None]
        for h in range(H):
            attn_da(s, i, h)
            attn_mla(s, i, h, (lambda h=h: da_epilogue(h)))
        if cfg.debug2 and q == 0:
            P.dma("pool", dbg["RH"], RH.rearrange("p c t -> p (c t)"), reads=b_RH, chan="dbg")
        if cfg.stop == 2:
            return
        norm_stats(xT, b_xT, NCH, D, nb())
        norm_apply(xT, b_xT, gcol["norm_mix"], NCH, RQ, b_RQ)
        for j in range(4):
            vga, bga = wtile(wb_in[:, O_GA + j * 512:O_GA + (j + 1) * 512], NCH, 512)
            vgb, bgb = wtile(wb_in[:, O_GB + j * 512:O_GB + (j + 1) * 512], NCH, 512)
            slot, bpp, ch = ring_next()
            vpa = slot[:, 0:4096].rearrange("p (c n) -> p c n", c=8)
            vpb = slot[:, 4096:8192].rearrange("p (c n) -> p c n", c=8)
            P.dma("sp", vpa, wb_pa[:, j * 512:(j + 1) * 512].rearrange("(c p) n -> p c n", p=128), reads=[b_wb],
                  writes=[bpp], chan=ch)
            P.dma("sp", vpb, wb_pb[:, j * 512:(j + 1) * 512].rearrange("(c p) n -> p c n", p=128), reads=[b_wb],
                  writes=[bpp], chan=ch)
            mb = 16 + (j % 2) * 4
            for cc in range(4):
                pa_, pb_, pc_, pd_ = nb(), nb(), nb(), nb()
                cs = slice(cc * 128, (cc + 1) * 128)
                for c in range(NCH):
                    P.op("pe", lambda e, c=c, cs=cs, pb=pa_, v=vga: e.matmul(
                        psum[pb], lhsT=v[:, c, cs], rhs=RQ[:, c, :], start=(c == 0), stop=(c == NCH - 1)),
                        reads=[bga, b_RQ[c]], writes=[b_ps[pa_]])
                for c in range(NCH):
                    P.op("pe", lambda e, c=c, cs=cs, pb=pb_, v=vgb: e.matmul(
                        psum[pb], lhsT=v[:, c, cs], rhs=RQ[:, c, :], start=(c == 0), stop=(c == NCH - 1)),
                        reads=[bgb, b_RQ[c]], writes=[b_ps[pb_]])
                for c in range(8):
                    P.op("pe", lambda e, c=c, cs=cs, pb=pc_, v=vpa: e.matmul(
                        psum[pb], lhsT=v[:, c, cs], rhs=RH[:, c, :], start=(c == 0), stop=(c == 7)),
                        reads=[bpp, b_RH[c]], writes=[b_ps[pc_]])
                for c in range(8):
                    P.op("pe", lambda e, c=c, cs=cs, pb=pd_, v=vpb: e.matmul(
                        psum[pb], lhsT=v[:, c, cs], rhs=RH[:, 8 + c, :], start=(c == 0), stop=(c == 7)),
                        reads=[bpp, b_RH[8 + c]], writes=[b_ps[pd_]])
                P.op("act", lambda e, pb=pa_: e.activation(out=scr[0], in_=psum[pb], func=AF.Sigmoid),
                     reads=[b_ps[pa_]], writes=[b_scr[0]])
                P.op("act", lambda e, pb=pb_: e.activation(out=scr[1], in_=psum[pb], func=AF.Sigmoid),
                     reads=[b_ps[pb_]], writes=[b_scr[1]])
                P.op("dve", lambda e, pb=pc_: e.tensor_tensor(out=scr[2], in0=psum[pb], in1=scr[0], op=ALU.mult),
                     reads=[b_ps[pc_], b_scr[0]], writes=[b_scr[2]])
                P.op("dve", lambda e, pb=pd_: e.tensor_tensor(out=scr[3], in0=psum[pb], in1=scr[1], op=ALU.mult),
                     reads=[b_ps[pd_], b_scr[1]], writes=[b_scr[3]])
                P.op("pool", lambda e, k=mb + cc: e.tensor_tensor(out=RQ[:, k, :], in0=scr[2], in1=scr[3], op=ALU.add),
                     reads=[b_scr[2], b_scr[3]], writes=[b_RQ[mb + cc]])
            vo, bo = wtile(wb_out[j * 512:(j + 1) * 512, :], 4, 2048)
            for oc in range(NCH):
                pb = nb()
                for c in range(4):
                    P.op("pe", lambda e, c=c, oc=oc, pb=pb, v=vo, mb=mb: e.matmul(
                        psum[pb], lhsT=v[:, c, oc * 128:(oc + 1) * 128], rhs=RQ[:, mb + c, :], start=(c == 0),
                        stop=(c == 3)), reads=[bo, b_RQ[mb + c]], writes=[b_ps[pb]])
                P.op("dve", lambda e, oc=oc, pb=pb: e.tensor_tensor(out=xT[:, oc, :], in0=xT[:, oc, :], in1=psum[pb],
                                                                   op=ALU.add),
                     reads=[b_ps[pb], b_xT[oc]], writes=[b_xT[oc]])
            if cfg.debug2 and q == 0:
                P.dma("pool", dbg["xj%d" % j], xT.rearrange("p c t -> p (c t)"), reads=b_xT, chan="dbg")
                P.dma("pool", dbg["mj%d" % j], RQ.rearrange("p c t -> p (c t)"), reads=b_RQ, chan="dbg")
        if cfg.debug2 and q == 0:
            P.dma("pool", dbg["x1"], xT.rearrange("p c t -> p (c t)"), reads=b_xT, chan="dbg")
            P.dma("pool", dbg["RQ2"], RQ.rearrange("p c t -> p (c t)"), reads=b_RQ, chan="dbg")
        if cfg.stop == 3:
            return
        norm_stats(xT, b_xT, NCH, D, nb())
        norm_apply(xT, b_xT, gcol["norm_ffn"], NCH, RQ, b_RQ)
        for j in range(NFJ):
            vg, bg = wtile(wb_gate[:, j * 512:(j + 1) * 512], NCH, 512)
            vu, bu = wtile(wb_up[:, j * 512:(j + 1) * 512], NCH, 512)
            ab = 16 + (j % 2) * 4
            for fc in range(4):
                pa_, pb_ = nb(), nb()
                cs = slice(fc * 128, (fc + 1) * 128)
                for c in range(NCH):
                    P.op("pe", lambda e, c=c, cs=cs, pb=pa_, v=vg: e.matmul(
                        psum[pb], lhsT=v[:, c, cs], rhs=RQ[:, c, :], start=(c == 0), stop=(c == NCH - 1)),
                        reads=[bg, b_RQ[c]], writes=[b_ps[pa_]])
                for c in range(NCH):
                    P.op("pe", lambda e, c=c, cs=cs, pb=pb_, v=vu: e.matmul(
                        psum[pb], lhsT=v[:, c, cs], rhs=RQ[:, c, :], start=(c == 0), stop=(c == NCH - 1)),
                        reads=[bu, b_RQ[c]], writes=[b_ps[pb_]])
                si = 4 + (fc % 2)
                P.op("act", lambda e, pb=pa_, si=si: e.activation(out=scr[si], in_=psum[pb], func=AF.Sigmoid),
                     reads=[b_ps[pa_]], writes=[b_scr[si]])
                P.op("dve", lambda e, pb=pa_, si=si: e.tensor_tensor(out=scr[si], in0=psum[pb], in1=scr[si], op=ALU.mult),
                     reads=[b_ps[pa_], b_scr[si]], writes=[b_scr[si]])
                P.op("dve", lambda e, pb=pb_, si=si, k=ab + fc: e.tensor_tensor(out=RQ[:, k, :], in0=psum[pb],
                                                                               in1=scr[si], op=ALU.mult),
                     reads=[b_ps[pb_], b_scr[si]], writes=[b_RQ[ab + fc]])
            vd, bd = wtile(wb_down[j * 512:(j + 1) * 512, :], 4, 2048)
            for oc in range(NCH):
                pb = nb()
                for c in range(4):
                    P.op("pe", lambda e, c=c, oc=oc, pb=pb, v=vd, ab=ab: e.matmul(
                        psum[pb], lhsT=v[:, c, oc * 128:(oc + 1) * 128], rhs=RQ[:, ab + c, :], start=(c == 0),
                        stop=(c == 3)), reads=[bd, b_RQ[ab + c]], writes=[b_ps[pb]])
                P.op("dve", lambda e, oc=oc, pb=pb: e.tensor_tensor(out=xT[:, oc, :], in0=xT[:, oc, :], in1=psum[pb],
                                                                   op=ALU.add),
                     reads=[b_ps[pb], b_xT[oc]], writes=[b_xT[oc]])
        if cfg.debug2 and q == 0:
            P.dma("pool", dbg["x2"], xT.rearrange("p c t -> p (c t)"), reads=b_xT, chan="dbg")
        if cfg.stop == 4:
            return
        norm_stats(xT, b_xT, NCH, D, nb())
        norm_apply(xT, b_xT, gcol["norm_final"], NCH, xT, b_xT)
        for b in range(4):
            bi = b % 2
            for c4 in range(NCH // 4):
                pb = nb()
                for k in range(4):
                    c = c4 * 4 + k
                    P.op("pe", lambda e, c=c, k=k, pb=pb, b=b: e.transpose(
                        out=psum[pb][:, k * 128:(k + 1) * 128], in_=xT[:, c, b * 128:(b + 1) * 128], identity=ident),
                        reads=[b_xT[c], b_const], writes=[b_ps[pb]])
                evac(xin[bi][:, c4 * 512:(c4 + 1) * 512], psum[pb], [b_ps[pb]], [b_xin[bi]])
            P.dma("pool", yout[s][i * QG + b * 128:i * QG + (b + 1) * 128, :], xin[bi], reads=[b_xin[bi]],
                  chan="yout%d" % bi)

    q = 0
    for s, nq in slots:
        for i in range(nq):
            qgroup(s, i, q)
            q += 1

    P.emit()
    return nc


def _rope_tables(S, off):
    inv = (10000.0 ** (-np.arange(0, 64, 2, dtype=np.float32) / np.float32(64))).astype(np.float32)
    pos = np.arange(S, dtype=np.float32)
    ang = (pos[:, None] * inv[None, :]).astype(np.float32)
    cos = np.concatenate([np.cos(ang), np.cos(ang)], axis=-1).astype(np.float32)
    sin = np.concatenate([np.sin(ang), np.sin(ang)], axis=-1).astype(np.float32)
    t = np.stack([cos.T, sin.T], axis=0)
    return np.ascontiguousarray(np.roll(t, -off, axis=2))


def _alibi_tables(S, off, nq):
    NKB = S // 128
    W = S - off
    slopes = 2.0 ** (-(np.arange(1, H + 1, dtype=np.float64)))
    out = np.zeros((nq, 2, NKB, H), np.float64)
    for i in range(nq):
        for r in range(NKB):
            if 4 * i <= r < 4 * i + 4:
                sig, offt = 1.0, 0.0
            elif r < 4 * i:
                sig, offt = 1.0, 512.0 * i - 128.0 * r
            elif 128 * r < W:
                sig, offt = -1.0, 128.0 * r - 512.0 * i
            else:
                sig, offt = 1.0, S + 512.0 * i - 128.0 * r
            out[i, 0, r, :] = -slopes * sig / DA_SCALE
            out[i, 1, r, :] = -slopes * offt
    return out.reshape(nq, 2, NKB * H).astype(np.float32)


def _cst_table():
    p = np.arange(128, dtype=np.float32)[:, None]
    q = np.arange(512, dtype=np.float32)[None, :]
    parts = [np.eye(128, dtype=np.float32), q - p]
    for d in range(4):
        parts.append(np.abs(q - p - 128.0 * d))
    return np.ascontiguousarray(np.concatenate(parts, axis=1).astype(np.float32))


def make_in_map(cfg, w, xA, offA, xB, offB):
    S = cfg.S
    m = {
        "xkvA": np.ascontiguousarray(np.roll(xA, -offA, axis=0)),
        "xkvB": np.ascontiguousarray(np.roll(xB, -offB, axis=0)),
        "ropeA": _rope_tables(S, offA),
        "ropeB": _rope_tables(S, offB),
        "alibi": np.ascontiguousarray(np.concatenate([_alibi_tables(S, offA, cfg.NQA),
                                                      _alibi_tables(S, offB, cfg.NQB)], axis=0)),
        "cst": _cst_table(),
        "w_in": w["w_in"][0], "w_uq": w["w_uq"][0], "w_ukv": w["w_ukv"][0],
        "w_proj_a": w["w_proj_a"][0], "w_proj_b": w["w_proj_b"][0], "w_out": w["w_out"][0],
        "w_gate": w["w_gate"][0], "w_up": w["w_up"][0], "w_down": w["w_down"][0],
        "norm_mix": w["norm_mix"].reshape(1, -1), "norm_ffn": w["norm_ffn"].reshape(1, -1),
        "norm_final": w["norm_final"].reshape(1, -1), "da_subln": w["da_subln"].reshape(1, -1),
        "mla_q_norm": w["mla_q_norm"].reshape(1, -1), "mla_kv_norm": w["mla_kv_norm"].reshape(1, -1),
        "lq1": w["da_lambda_q1"].reshape(1, -1), "lk1": w["da_lambda_k1"].reshape(1, -1),
        "lq2": w["da_lambda_q2"].reshape(1, -1), "lk2": w["da_lambda_k2"].reshape(1, -1),
    }
    return {k: np.ascontiguousarray(v, dtype=np.float32) for k, v in m.items()}


_NC_CACHE = {}


def _plan():
    plan = []
    for c in range(8):
        G0 = 12 * c
        first_seq, last_seq = G0 // 32, (G0 + 11) // 32
        if (G0 + 7) // 32 == first_seq:
            a = (first_seq, G0 % 32)
            b = ((G0 + 8) // 32, (G0 + 8) % 32)
        else:
            b = (first_seq, G0 % 32)
            a = ((G0 + 4) // 32, (G0 + 4) % 32)
        plan.append((a, b))
    return plan


def kernel(x_prompt, x_sample, norm_mix, w_in, da_lambda_q1, da_lambda_k1, da_lambda_q2, da_lambda_k2, da_subln,
           mla_q_norm, w_uq, mla_kv_norm, w_ukv, w_proj_a, w_proj_b, w_out, norm_ffn, w_gate, w_up, w_down,
           norm_final):
    w = dict(norm_mix=norm_mix, w_in=w_in, da_lambda_q1=da_lambda_q1, da_lambda_k1=da_lambda_k1,
             da_lambda_q2=da_lambda_q2, da_lambda_k2=da_lambda_k2, da_subln=da_subln, mla_q_norm=mla_q_norm,
             w_uq=w_uq, mla_kv_norm=mla_kv_norm, w_ukv=w_ukv, w_proj_a=w_proj_a, w_proj_b=w_proj_b, w_out=w_out,
             norm_ffn=norm_ffn, w_gate=w_gate, w_up=w_up, w_down=w_down, norm_final=norm_final)
    w = {k: np.asarray(v, dtype=np.float32) for k, v in w.items()}
    x_prompt = np.asarray(x_prompt, dtype=np.float32)
    x_sample = np.asarray(x_sample, dtype=np.float32)
    seqs = [x_prompt[0], x_prompt[1], x_sample[0]]
    S = seqs[0].shape[0]
    cfg = Cfg(S=S, NQA=8, NQB=4)
    if "nc" not in _NC_CACHE:
        _NC_CACHE["nc"] = build_nc(cfg)
    nc = _NC_CACHE["nc"]
    plan = _plan()
    in_maps = []
    for (sa, ga), (sb_, gb) in plan:
        in_maps.append(make_in_map(cfg, w, seqs[sa], ga * QG, seqs[sb_], gb * QG))
    res = run_bass_kernel_spmd(nc, in_maps, core_ids=list(range(8)))
    outs = [np.empty((S, D), np.float32) for _ in range(3)]
    for c, ((sa, ga), (sb_, gb)) in enumerate(plan):
        r = res.results[c]
        outs[sa][ga * QG:(ga + 8) * QG] = r["yA"]
        outs[sb_][gb * QG:(gb + 4) * QG] = r["yB"]
    y_prompt = np.stack([outs[0], outs[1]], axis=0)
    y_sample = outs[2][None]
    return (y_prompt, y_sample)
```

```python
import math
from contextlib import ExitStack

import numpy as np
import ml_dtypes

import concourse.bass as bass
import concourse.mybir as mybir
from concourse.bass_utils import run_bass_kernel_spmd

F32 = mybir.dt.float32
BF16 = mybir.dt.bfloat16
AF = mybir.ActivationFunctionType
ALU = mybir.AluOpType

D = 2048
NCH = D // 128
H = 8
DFF = 5632
NFJ = DFF // 512
INW = 8000
QG = 512
EPS = 1e-6
LAMBDA_INIT = 0.8 - 0.6 * math.exp(0.0)
DA_SCALE = 64 ** -0.5
MLA_SCALE = 192 ** -0.5
O_Q, O_K, O_V, O_CQ, O_CKV, O_KR, O_GA, O_GB = 0, 1024, 2048, 3072, 3584, 3840, 3904, 5952


class Buf:
    __slots__ = ("name", "writer", "readers")

    def __init__(self, name):
        self.name = name
        self.writer = None
        self.readers = {}


class Inst:
    __slots__ = ("eng", "fn", "pos", "is_dma", "chan", "count", "marked", "waits")

    def __init__(self, eng, fn, pos, is_dma=False, chan=None):
        self.eng = eng
        self.fn = fn
        self.pos = pos
        self.is_dma = is_dma
        self.chan = chan
        self.count = None
        self.marked = False
        self.waits = []


class Prog:
    ENGS = ("pe", "act", "dve", "pool", "sp")

    def __init__(self, nc):
        self.nc = nc
        self.streams = {e: [] for e in self.ENGS}
        self.waited = {}
        self.chan_last = {}
        self.chan_n = {}
        self.chans = []
        self.ndma = 0

    def _dep(self, I, P):
        if P is None or P is I:
            return
        if P.is_dma:
            key = (I.eng, "c", P.chan)
            if self.waited.get(key, 0) >= P.count:
                return
            self.waited[key] = P.count
            I.waits.append(P)
            return
        if P.eng == "pe" and I.eng == "pe" and not I.is_dma:
            return
        key = (I.eng, "e", P.eng)
        if self.waited.get(key, -1) >= P.pos:
            return
        self.waited[key] = P.pos
        P.marked = True
        I.waits.append(P)

    def _track(self, I, reads, writes):
        for b in reads:
            self._dep(I, b.writer)
        for b in writes:
            self._dep(I, b.writer)
            for R in b.readers.values():
                self._dep(I, R)
        for b in reads:
            if I.is_dma:
                b.readers[("dma", id(I))] = I
            else:
                b.readers[I.eng] = I
        for b in writes:
            b.writer = I
            b.readers = {}

    def op(self, eng, fn, reads=(), writes=()):
        st = self.streams[eng]
        I = Inst(eng, fn, len(st))
        st.append(I)
        self._track(I, reads, writes)
        return I

    def dma(self, eng, out, in_, reads=(), writes=(), chan=None, **kw):
        st = self.streams[eng]
        if chan is None:
            chan = "auto%d" % (self.ndma % 8)
        self.ndma += 1
        if chan not in self.chan_n:
            self.chan_n[chan] = 0
            self.chans.append(chan)
        I = Inst(eng, (lambda e, o=out, i=in_, k=kw: e.dma_start(out=o, in_=i, **k)), len(st), True, chan)
        st.append(I)
        self._dep(I, self.chan_last.get(chan))
        self.chan_n[chan] += 1
        I.count = 16 * self.chan_n[chan]
        self.chan_last[chan] = I
        self._track(I, reads, writes)
        return I

    def emit(self):
        nc = self.nc
        for chan in self.chans:
            I = Inst("sp", None, len(self.streams["sp"]))
            self.streams["sp"].append(I)
            self._dep(I, self.chan_last[chan])
        for e in ("pe", "act", "dve", "pool"):
            c = 0
            for I in self.streams[e]:
                if I.marked and not I.is_dma:
                    c += 1
                    I.count = c
        with ExitStack() as es:
            esem = {e: es.enter_context(nc.semaphore("s_" + e)) for e in ("pe", "act", "dve", "pool")}
            csem = {c: es.enter_context(nc.semaphore("c_%d" % i)) for i, c in enumerate(self.chans)}
            block = es.enter_context(nc.Block())

            def run(stream):
                def f(eng):
                    for I in stream:
                        for P in I.waits:
                            if P.is_dma:
                                eng.wait_ge(csem[P.chan], P.count)
                            else:
                                eng.wait_ge(esem[P.eng], P.count)
                        if I.fn is None:
                            continue
                        bi = I.fn(eng)
                        if I.is_dma:
                            bi.then_inc(csem[I.chan], 16)
                        elif I.marked:
                            bi.then_inc(esem[I.eng], 1)
                return f

            block.tensor(run(self.streams["pe"]))
            block.scalar(run(self.streams["act"]))
            block.vector(run(self.streams["dve"]))
            block.gpsimd(run(self.streams["pool"]))
            block.sync(run(self.streams["sp"]))


class Cfg:
    def __init__(self, S=16384, NQA=8, NQB=4, debug=False, main=True, debug2=False, stop=99):
        self.S = S
        self.NQA = NQA
        self.NQB = NQB
        self.NT = S // QG
        self.NKB = S // 128
        self.KCH = 2048 if S >= 2048 else S
        self.debug = debug
        self.main = main
        self.stop = stop
        self.debug2 = debug2


def build_nc(cfg):
    nc = bass.Bass("TRN2", target_bir_lowering=False)
    P = Prog(nc)
    S, NT, NKB = cfg.S, cfg.NT, cfg.NKB
    NQ = cfg.NQA + cfg.NQB
    slots = [("A", cfg.NQA), ("B", cfg.NQB)]

    def din(name, shape, dt=F32):
        return nc.dram_tensor(name, list(shape), dt, kind="ExternalInput").ap()

    def dout(name, shape, dt=F32):
        return nc.dram_tensor(name, list(shape), dt, kind="ExternalOutput").ap()

    def dscr(name, shape, dt=BF16):
        return nc.dram_tensor(name, list(shape), dt).ap()

    xkv = {"A": din("xkvA", [S, D]), "B": din("xkvB", [S, D])}
    rope = {"A": din("ropeA", [2, 64, S]), "B": din("ropeB", [2, 64, S])}
    alibi = din("alibi", [NQ, 2, NKB * H])
    cst = din("cst", [128, 128 + 512 * 5])
    augk = din("augk", [NQ * H * 3, S])
    qaug = din("qaug", [3, QG])
    w_in = din("w_in", [D, INW])
    w_uq = din("w_uq", [512, 1536])
    w_ukv = din("w_ukv", [256, 2048])
    w_pa = din("w_proj_a", [1024, D])
    w_pb = din("w_proj_b", [1024, D])
    w_out = din("w_out", [D, D])
    w_gate = din("w_gate", [D, DFF])
    w_up = din("w_up", [D, DFF])
    w_down = din("w_down", [DFF, D])
    vec = {n: din(n, [1, sz]) for n, sz in (("norm_mix", D), ("norm_ffn", D), ("norm_final", D),
                                            ("da_subln", 128), ("mla_q_norm", 512), ("mla_kv_norm", 256),
                                            ("lq1", 64), ("lk1", 64), ("lq2", 64), ("lk2", 64))}
    yout = {"A": dout("yA", [cfg.NQA * QG, D]), "B": dout("yB", [cfg.NQB * QG, D])}

    wb_in = dscr("wb_in", [D, INW])
    wb_uqx = dscr("wb_uqx", [512, 2048])
    wb_pa = dscr("wb_pa", [1024, D])
    wb_pb = dscr("wb_pb", [1024, D])
    wb_out = dscr("wb_out", [D, D])
    wb_gate = dscr("wb_gate", [D, DFF])
    wb_up = dscr("wb_up", [D, DFF])
    wb_down = dscr("wb_down", [DFF, D])
    kv = {}
    for s, _ in slots:
        kv[s] = dict(
            kda=dscr("kda" + s, [H, 128, S]),
            vda=dscr("vda" + s, [128, H, NKB, 128]),
            kn=dscr("kn" + s, [H, 128, S]),
            kr=dscr("kr" + s, [64, S]),
            vb=dscr("vb" + s, [128, H, NKB, 128]),
        )
    augk_b = dscr("augk_b", [NQ * H * 3, S])
    b_wb = Buf("wb")
    b_kv = {s: Buf("kv" + s) for s, _ in slots}

    dbg = {}
    if cfg.debug2:
        dbg["RQ"] = dout("dbg_RQ", [128, 24 * QG], BF16)
        dbg["RH"] = dout("dbg_RH", [128, NCH * QG], BF16)
        dbg["x1"] = dout("dbg_x1", [128, NCH * QG], F32)
        dbg["RQ2"] = dout("dbg_RQ2", [128, 24 * QG], BF16)
        for jj in range(4):
            dbg["xj%d" % jj] = dout("dbg_xj%d" % jj, [128, NCH * QG], F32)
            dbg["mj%d" % jj] = dout("dbg_mj%d" % jj, [128, 24 * QG], BF16)
        dbg["x2"] = dout("dbg_x2", [128, NCH * QG], F32)
    if cfg.debug:
        dbg["kda"] = dout("dbg_kda", [H, 128, S], BF16)
        dbg["vda"] = dout("dbg_vda", [128, H, NKB, 128], BF16)
        dbg["kn"] = dout("dbg_kn", [H, 128, S], BF16)
        dbg["kr"] = dout("dbg_kr", [64, S], BF16)
        dbg["vb"] = dout("dbg_vb", [128, H, NKB, 128], BF16)

    U8 = mybir.dt.uint8
    ARENA = 206 * 1024
    arena = nc.alloc_sbuf_tensor("arena", [128, ARENA], U8).ap()
    DTSZ = {F32: 4, BF16: 2}

    class Arena:
        def __init__(self, base, limit):
            self.off = base
            self.limit = limit

        def alloc(self, shape, dt, parts=128):
            n = int(np.prod(shape)) * DTSZ[dt]
            off = self.off
            self.off += (n + 63) // 64 * 64
            assert self.off <= self.limit, (self.off, self.limit)
            v = arena[0:parts, off:off + n].bitcast(dt)
            if len(shape) == 2:
                v = v.rearrange("p (a b) -> p a b", a=shape[0])
            elif len(shape) == 3:
                v = v.rearrange("p (a b c) -> p a b c", a=shape[0], b=shape[1])
            return v

    SH = Arena(0, ARENA)
    psum = [nc.alloc_psum_tensor("ps%d" % i, [128, 512], F32).ap() for i in range(8)]
    b_ps = [Buf("ps%d" % i) for i in range(8)]

    ident = SH.alloc([128], F32)
    ones_f = SH.alloc([128], F32)
    ones_b = SH.alloc([128], BF16)
    b_const = Buf("const")
    gcol = {}
    for n, sz in (("norm_mix", D), ("norm_ffn", D), ("norm_final", D), ("da_subln", 128),
                  ("mla_q_norm", 512), ("mla_kv_norm", 256)):
        gcol[n] = SH.alloc([sz // 128], F32)
    gsub08 = SH.alloc([1], F32)
    lam_t = SH.alloc([4, 64], F32)
    lam_s = SH.alloc([4], F32)
    neglam = SH.alloc([1], F32)
    xin = [SH.alloc([D], F32) for i in range(2)]
    b_xin = [Buf("xin%d" % i) for i in range(2)]
    xT = SH.alloc([NCH, QG], F32)
    b_xT = [Buf("xT%d" % c) for c in range(NCH)]
    RH = SH.alloc([NCH, QG], BF16)
    b_RH = [Buf("RH%d" % c) for c in range(NCH)]
    sq = [SH.alloc([QG], F32) for i in range(2)]
    b_sq = [Buf("sq%d" % i) for i in range(2)]
    rstd = SH.alloc([QG], F32)
    b_rstd = Buf("rstd")
    PH_BASE = SH.off

    P.dma("sp", ident, cst[:, 0:128], writes=[b_const], chan="cst")
    for n in gcol:
        P.dma("sp", gcol[n], vec[n].rearrange("o (c p) -> p (o c)", p=128), writes=[b_const], chan="cst",
              allow_slow_non_contiguous=True)
    for i, n in enumerate(("lq1", "lk1", "lq2", "lk2")):
        P.dma("sp", lam_t[:, i, :], vec[n].partition_broadcast(128), writes=[b_const], chan="cst")
    P.op("pool", lambda e: e.memset(ones_f, 1.0), writes=[b_const])
    P.op("pool", lambda e: e.memset(ones_b, 1.0), writes=[b_const])
    b_lam = Buf("lam")
    P.op("dve", lambda e: e.tensor_scalar(out=gsub08, in0=gcol["da_subln"], scalar1=1.0 - LAMBDA_INIT, scalar2=None,
                                          op0=ALU.mult), reads=[b_const], writes=[b_lam])
    P.op("dve", lambda e: e.tensor_tensor(out=lam_t[:, 0, :], in0=lam_t[:, 0, :], in1=lam_t[:, 1, :], op=ALU.mult),
         reads=[b_const, b_lam], writes=[b_lam])
    P.op("dve", lambda e: e.tensor_tensor(out=lam_t[:, 2, :], in0=lam_t[:, 2, :], in1=lam_t[:, 3, :], op=ALU.mult),
         reads=[b_const, b_lam], writes=[b_lam])
    P.op("dve", lambda e: e.reduce_sum(out=lam_s[:, 0:1], in_=lam_t[:, 0, :], axis=mybir.AxisListType.X),
         reads=[b_lam], writes=[b_lam])
    P.op("dve", lambda e: e.reduce_sum(out=lam_s[:, 1:2], in_=lam_t[:, 2, :], axis=mybir.AxisListType.X),
         reads=[b_lam], writes=[b_lam])
    P.op("act", lambda e: e.activation(out=lam_s[:, 2:4], in_=lam_s[:, 0:2], func=AF.Exp),
         reads=[b_lam], writes=[b_lam])
    P.op("dve", lambda e: e.tensor_tensor(out=neglam, in0=lam_s[:, 3:4], in1=lam_s[:, 2:3], op=ALU.subtract),
         reads=[b_lam], writes=[b_lam])
    P.op("dve", lambda e: e.tensor_scalar(out=neglam, in0=neglam, scalar1=-LAMBDA_INIT, scalar2=None, op0=ALU.add),
         reads=[b_lam], writes=[b_lam])

    cc_n = [0]

    def cast_copy(dst, src, rows):
        RB = 128
        cols = src.shape[1]
        a = 1
        while cols // a > 2048 or cols % a:
            a += 1
        for r0 in range(0, rows, RB):
            r1 = min(rows, r0 + RB)
            P.dma("pool", dst[r0:r1, :].rearrange("r (a b) -> r a b", a=a),
                  src[r0:r1, :].rearrange("r (a b) -> r a b", a=a), writes=[b_wb], chan="wcast%d" % (cc_n[0] % 4))
            cc_n[0] += 1


    ev_cnt = [0]

    def evac(dst, src, reads, writes):
        ev_cnt[0] += 1
        if ev_cnt[0] % 2 == 0:
            P.op("act", lambda e: e.copy(out=dst, in_=src), reads=reads, writes=writes)
        else:
            P.op("dve", lambda e: e.tensor_copy(out=dst, in_=src), reads=reads, writes=writes)

    def load_xT(src_rows):
        for b in range(4):
            bi = b % 2
            P.dma("sp", xin[bi], src_rows[b * 128:(b + 1) * 128, :], writes=[b_xin[bi]], chan="xin%d" % bi)
            for c4 in range(NCH // 4):
                pb = c4 % 2
                for k in range(4):
                    c = c4 * 4 + k
                    P.op("pe", lambda e, c=c, k=k, pb=pb, bi=bi: e.transpose(
                        out=psum[pb][:, k * 128:(k + 1) * 128], in_=xin[bi][:, c * 128:(c + 1) * 128], identity=ident),
                        reads=[b_xin[bi], b_const], writes=[b_ps[pb]])
                evac(xT[:, c4 * 4:(c4 + 1) * 4, b * 128:(b + 1) * 128], psum[pb].rearrange("p (k t) -> p k t", k=4),
                     [b_ps[pb]], b_xT[c4 * 4:(c4 + 1) * 4])

    def norm_stats(src, b_src, nchunks, width, pbank, parts=128):
        for c in range(nchunks):
            si = c % 2
            P.op("act", lambda e, c=c, si=si: e.activation(out=sq[si], in_=src[:, c, :], func=AF.Square),
                 reads=[b_src[c]], writes=[b_sq[si]])
            P.op("pe", lambda e, c=c, si=si: e.matmul(psum[pbank], lhsT=ones_f, rhs=sq[si], start=(c == 0),
                                                      stop=(c == nchunks - 1)),
                 reads=[b_sq[si], b_const], writes=[b_ps[pbank]])
        P.op("act", lambda e: e.activation(out=rstd, in_=psum[pbank], func=AF.Sqrt, scale=1.0 / width, bias=EPS),
             reads=[b_ps[pbank]], writes=[b_rstd])
        P.op("dve", lambda e: e.reciprocal(out=rstd, in_=rstd), reads=[b_rstd], writes=[b_rstd])

    def norm_apply(src, b_src, g, nchunks, dst, b_dst, engs=("dve",)):
        for c in range(nchunks):
            eng = engs[c % len(engs)]
            P.op(eng, lambda e, c=c: e.scalar_tensor_tensor(out=dst[:, c, :], in0=src[:, c, :], scalar=g[:, c:c + 1],
                                                            in1=rstd, op0=ALU.mult, op1=ALU.mult),
                 reads=[b_src[c], b_rstd, b_const], writes=[b_dst[c]])

    def barrier():
        lasts = {e: (P.streams[e][-1] if P.streams[e] else None) for e in ("pe", "act", "dve", "pool")}
        chans = list(P.chans)
        for x in ("pe", "act", "dve", "pool", "sp"):
            I = Inst(x, None, len(P.streams[x]))
            P.streams[x].append(I)
            for e, L in lasts.items():
                if L is not None and e != x:
                    key = (x, "e", e)
                    if P.waited.get(key, -1) < L.pos:
                        P.waited[key] = L.pos
                        L.marked = True
                        I.waits.append(L)
            for c in chans:
                P._dep(I, P.chan_last[c])

    if cfg.main:
        XB = xT.rearrange("p c t -> p (c t)").bitcast(BF16)
        uq_src = XB[:, 0:4 * 1536].rearrange("p (c n) -> p c n", c=4)
        uq_ext = XB[:, 6144:6144 + 4 * 2048].rearrange("p (c n) -> p c n", c=4)
        P.dma("pool", uq_src, w_uq.rearrange("(c p) n -> p c n", p=128), writes=b_xT, chan="wkv0")
        for c in range(4):
            sv = uq_src[:, c, :].rearrange("p (h d) -> p h d", h=H)
            dv = uq_ext[:, c, :].rearrange("p (h d) -> p h d", h=H)
            P.op("dve", lambda e, sv=sv, dv=dv: e.tensor_copy(out=dv[:, :, 0:192], in_=sv[:, :, 0:192]),
                 reads=b_xT, writes=b_xT)
            P.op("dve", lambda e, sv=sv, dv=dv: e.tensor_scalar(out=dv[:, :, 192:224], in0=sv[:, :, 160:192], scalar1=-1.0,
                                                                scalar2=None, op0=ALU.mult), reads=b_xT, writes=b_xT)
            P.op("dve", lambda e, sv=sv, dv=dv: e.tensor_copy(out=dv[:, :, 224:256], in_=sv[:, :, 128:160]),
                 reads=b_xT, writes=b_xT)
        P.dma("pool", wb_uqx.rearrange("(c p) n -> p c n", p=128), uq_ext, reads=b_xT, writes=[b_wb], chan="wkv0")
        cast_copy(augk_b, augk, NQ * H * 3)
        cast_copy(wb_in, w_in, D)
        cast_copy(wb_pa, w_pa, 1024)
        cast_copy(wb_pb, w_pb, 1024)
        cast_copy(wb_out, w_out, D)
        cast_copy(wb_gate, w_gate, D)
        cast_copy(wb_up, w_up, D)
        cast_copy(wb_down, w_down, DFF)

    A1 = Arena(PH_BASE, ARENA)
    wk = A1.alloc([NCH, 1024], BF16)
    wv = A1.alloc([NCH, 1024], BF16)
    wc = A1.alloc([NCH, 384], BF16)
    wu = A1.alloc([2, 2048], BF16)
    b_wkv = Buf("wkv")
    w_in_v = w_in.rearrange("(c p) n -> p c n", p=128)
    for c in range(NCH):
        P.dma("pool", wk[:, c, :], w_in_v[:, c, O_K:O_K + 1024], writes=[b_wkv], chan="wkv0")
        P.dma("pool", wv[:, c, :], w_in_v[:, c, O_V:O_V + 1024], writes=[b_wkv], chan="wkv1")
    P.dma("pool", wc[:, :, 0:320], w_in_v[:, :, O_CKV:O_CKV + 320], writes=[b_wkv], chan="wkv2")
    P.dma("pool", wc[:, :, 320:352], w_in_v[:, :, O_KR + 32:O_KR + 64], writes=[b_wkv], chan="wkv2")
    P.dma("pool", wc[:, :, 352:384], w_in_v[:, :, O_KR:O_KR + 32], writes=[b_wkv], chan="wkv2")
    P.op("dve", lambda e: e.tensor_scalar(out=wc[:, :, 320:352], in0=wc[:, :, 320:352], scalar1=-1.0, scalar2=None,
                                          op0=ALU.mult), reads=[b_wkv], writes=[b_wkv])
    w_ukv_v = w_ukv.rearrange("(c p) (h t d) -> p c h t d", p=128, h=H, t=2)
    for c in range(2):
        for t in range(2):
            P.dma("pool", wu[:, c, t * 1024:(t + 1) * 1024].rearrange("p (h d) -> p h d", h=H),
                  w_ukv_v[:, c, :, t, :], writes=[b_wkv], chan="wkv3")

    ckvT = A1.alloc([2, QG], F32)
    b_ckvT = [Buf("ckvT0"), Buf("ckvT1")]
    ckvn = A1.alloc([2, QG], BF16)
    b_ckvn = [Buf("ckvn0"), Buf("ckvn1")]
    krT = A1.alloc([2, QG], F32, parts=64)
    b_krT = [Buf("krT0"), Buf("krT1")]
    ropet = A1.alloc([2, QG], F32, parts=64)
    b_ropet = Buf("ropet")
    krtmp = A1.alloc([QG], F32, parts=64)
    b_krtmp = Buf("krtmp")
    NST = 3
    stK = [A1.alloc([QG], BF16) for i in range(NST)]
    b_stK = [Buf("stK%d" % i) for i in range(NST)]
    stV = [A1.alloc([H, 4, 128], BF16) for i in range(2)]
    b_stV = [[Buf("stV%d_%d" % (i, j)) for j in range(8)] for i in range(2)]
    stR = A1.alloc([QG], BF16, parts=64)
    b_stR = Buf("stR")
    g_kv = gcol["mla_kv_norm"]
    hT = RH
    b_hT = b_RH
    stk_i = [0]

    def kv_tile(s, t):
        sc = kv[s]
        load_xT(xkv[s][t * QG:(t + 1) * QG, :])
        P.dma("sp", ropet, rope[s][:, :, t * QG:(t + 1) * QG].rearrange("a p t -> p a t"), writes=[b_ropet],
              chan="ropet")
        norm_stats(xT, b_xT, NCH, D, 2)
        norm_apply(xT, b_xT, gcol["norm_mix"], NCH, hT, b_hT)
        for h in range(H):
            pb = 3 + (h % 2)
            for c in range(NCH):
                P.op("pe", lambda e, c=c, h=h, pb=pb: e.matmul(psum[pb], lhsT=wk[:, c, h * 128:(h + 1) * 128],
                                                              rhs=hT[:, c, :], start=(c == 0), stop=(c == NCH - 1)),
                     reads=[b_wkv, b_hT[c]], writes=[b_ps[pb]])
            si = stk_i[0] % NST
            stk_i[0] += 1
            evac(stK[si], psum[pb], [b_ps[pb]], [b_stK[si]])
            P.dma("pool", sc["kda"][h, :, t * QG:(t + 1) * QG], stK[si], reads=[b_stK[si]], writes=[b_kv[s]],
                  chan="stK%d" % si)
        vi = 0
        for blk in range(4):
            for half in range(2):
                pb = 5 + half
                for c in range(NCH):
                    P.op("pe", lambda e, c=c, blk=blk, half=half, pb=pb: e.matmul(
                        psum[pb], lhsT=hT[:, c, blk * 128:(blk + 1) * 128], rhs=wv[:, c, half * 512:(half + 1) * 512],
                        start=(c == 0), stop=(c == NCH - 1)), reads=[b_wkv, b_hT[c]], writes=[b_ps[pb]])
                evac(stV[vi][:, half * 4:(half + 1) * 4, blk, :], psum[pb].rearrange("p (h d) -> p h d", h=4),
                     [b_ps[pb]], [b_stV[vi][blk * 2 + half]])
        P.dma("pool", sc["vda"][:, :, t * 4:(t + 1) * 4, :], stV[vi], reads=b_stV[vi], writes=[b_kv[s]],
              chan="stV%d" % vi)
        for j in range(2):
            pb = 3 + j
            for c in range(NCH):
                P.op("pe", lambda e, c=c, j=j, pb=pb: e.matmul(psum[pb], lhsT=wc[:, c, j * 128:(j + 1) * 128],
                                                              rhs=hT[:, c, :], start=(c == 0), stop=(c == NCH - 1)),
                     reads=[b_wkv, b_hT[c]], writes=[b_ps[pb]])
            evac(ckvT[:, j, :], psum[pb], [b_ps[pb]], [b_ckvT[j]])
        for j in range(2):
            pb = 5 + j
            for c in range(NCH):
                P.op("pe", lambda e, c=c, j=j, pb=pb: e.matmul(psum[pb][0:64, :], lhsT=wc[:, c, 256 + j * 64:320 + j * 64],
                                                              rhs=hT[:, c, :], start=(c == 0), stop=(c == NCH - 1)),
                     reads=[b_wkv, b_hT[c]], writes=[b_ps[pb]])
            evac(krT[:, j, :], psum[pb][0:64, :], [b_ps[pb]], [b_krT[j]])
        P.op("dve", lambda e: e.tensor_tensor(out=krtmp, in0=krT[:, 0, :], in1=ropet[:, 0, :], op=ALU.mult),
             reads=[b_krT[0], b_ropet], writes=[b_krtmp])
        P.op("dve", lambda e: e.tensor_tensor(out=krT[:, 1, :], in0=krT[:, 1, :], in1=ropet[:, 1, :], op=ALU.mult),
             reads=[b_krT[1], b_ropet], writes=[b_krT[1]])
        P.op("dve", lambda e: e.tensor_tensor(out=stR, in0=krtmp, in1=krT[:, 1, :], op=ALU.add),
             reads=[b_krT[1], b_krtmp], writes=[b_stR])
        P.dma("pool", sc["kr"][:, t * QG:(t + 1) * QG], stR, reads=[b_stR], writes=[b_kv[s]], chan="stR")
        norm_stats(ckvT, b_ckvT, 2, 256, 2)
        norm_apply(ckvT, b_ckvT, g_kv, 2, ckvn, b_ckvn)
        for h in range(H):
            pb = 3 + (h % 2)
            for c in range(2):
                P.op("pe", lambda e, c=c, h=h, pb=pb: e.matmul(psum[pb], lhsT=wu[:, c, h * 128:(h + 1) * 128],
                                                              rhs=ckvn[:, c, :], start=(c == 0), stop=(c == 1)),
                     reads=[b_wkv, b_ckvn[c]], writes=[b_ps[pb]])
            si = stk_i[0] % NST
            stk_i[0] += 1
            evac(stK[si], psum[pb], [b_ps[pb]], [b_stK[si]])
            P.dma("pool", sc["kn"][h, :, t * QG:(t + 1) * QG], stK[si], reads=[b_stK[si]], writes=[b_kv[s]],
                  chan="stK%d" % si)
        vi2 = 1
        for blk in range(4):
            for half in range(2):
                pb = 5 + half
                for c in range(2):
                    P.op("pe", lambda e, c=c, blk=blk, half=half, pb=pb: e.matmul(
                        psum[pb], lhsT=ckvn[:, c, blk * 128:(blk + 1) * 128],
                        rhs=wu[:, c, 1024 + half * 512:1024 + (half + 1) * 512], start=(c == 0), stop=(c == 1)),
                        reads=[b_wkv, b_ckvn[c]], writes=[b_ps[pb]])
                evac(stV[vi2][:, half * 4:(half + 1) * 4, blk, :], psum[pb].rearrange("p (h d) -> p h d", h=4),
                     [b_ps[pb]], [b_stV[vi2][blk * 2 + half]])
        P.dma("pool", sc["vb"][:, :, t * 4:(t + 1) * 4, :], stV[vi2], reads=b_stV[vi2], writes=[b_kv[s]],
              chan="stV%d" % vi2)

    for s, _ in slots:
        for t in range(NT):
            kv_tile(s, t)

    if cfg.debug:
        for k in ("kda", "vda", "kn", "kr", "vb"):
            P.dma("sp", dbg[k], kv["A"][k], reads=[b_kv["A"]], chan="dbg")

    if not cfg.main:
        P.emit()
        return nc

    barrier()
    if cfg.stop == 0:
        P.emit()
        return nc
    A2 = Arena(PH_BASE, ARENA)
    NSLOT = 3
    ring = [A2.alloc([8192], BF16) for _ in range(NSLOT)]
    b_ring = [Buf("ring%d" % k) for k in range(NSLOT)]
    RQ = A2.alloc([24, QG], BF16)
    b_RQ = [Buf("RQ%d" % c) for c in range(24)]
    NPT = 6
    Pt = [A2.alloc([QG], BF16) for _ in range(NPT)]
    b_Pt = [Buf("Pt%d" % k) for k in range(NPT)]
    tmpS = [A2.alloc([QG], F32) for _ in range(4)]
    b_tmpS = [Buf("tmpS%d" % k) for k in range(4)]
    Dt = A2.alloc([5, QG], F32)
    b_Dt = Buf("Dt")
    c1tab = A2.alloc([NKB * H], F32)
    btab = A2.alloc([NKB * H], F32)
    b_tab = Buf("tab")
    ropeq = A2.alloc([2, QG], F32, parts=64)
    b_ropeq = Buf("ropeq")
    scr = [A2.alloc([QG], F32) for _ in range(6)]
    b_scr = [Buf("scr%d" % k) for k in range(6)]
    qaug_t = A2.alloc([QG], BF16)
    P.dma("pool", qaug_t[0:3, :], qaug, writes=[b_Dt], chan="wkv0")
    P.dma("pool", qaug_t[32:35, :], qaug, writes=[b_Dt], chan="wkv0")
    cqT = xin[0].rearrange("p (c t) -> p c t", c=4)
    cqn = xin[1][:, 0:1024].bitcast(BF16).rearrange("p (c t) -> p c t", c=4)
    qrt = xin[1][0:64, 1024:2048].rearrange("p (c t) -> p c t", c=2)
    KCH = cfg.KCH
    KB = KCH // 128
    NCHK = S // KCH

    P.dma("sp", Dt, cst[:, 128:128 + 5 * QG].rearrange("p (a t) -> p a t", a=5), writes=[b_Dt], chan="cst")

    ring_n = [0]

    def ring_next():
        k = ring_n[0] % NSLOT
        ring_n[0] += 1
        return ring[k], b_ring[k], "ring%d" % k

    bank_n = [0]

    def nb():
        bank_n[0] += 1
        return bank_n[0] % 8

    def wtile(src2d, kc, ncols):
        slot, bf, ch = ring_next()
        view = slot[:, 0:kc * ncols].rearrange("p (c n) -> p c n", c=kc)
        P.dma("sp", view, src2d.rearrange("(c p) n -> p c n", p=128), reads=[b_wb], writes=[bf], chan=ch)
        return view, bf

    def da_epilogue(h):
        P.op("dve", lambda e: e.reciprocal(out=scr[0], in_=psum[2]), reads=[b_ps[2]], writes=[b_scr[0]])
        P.op("dve", lambda e: e.tensor_tensor(out=scr[1], in0=psum[0], in1=scr[0], op=ALU.mult),
             reads=[b_ps[0], b_scr[0]], writes=[b_scr[1]])
        P.op("dve", lambda e: e.reciprocal(out=scr[2], in_=psum[3]), reads=[b_ps[3]], writes=[b_scr[2]])
        P.op("dve", lambda e: e.tensor_tensor(out=scr[3], in0=psum[1], in1=scr[2], op=ALU.mult),
             reads=[b_ps[1], b_scr[2]], writes=[b_scr[3]])
        P.op("dve", lambda e: e.scalar_tensor_tensor(out=scr[4], in0=scr[3], scalar=neglam, in1=scr[1],
                                                     op0=ALU.mult, op1=ALU.add),
             reads=[b_scr[3], b_scr[1], b_lam], writes=[b_scr[4]])
        P.op("act", lambda e: e.activation(out=sq[0], in_=scr[4], func=AF.Square), reads=[b_scr[4]], writes=[b_sq[0]])
        P.op("pe", lambda e: e.matmul(psum[0], lhsT=ones_f, rhs=sq[0], start=True, stop=True),
             reads=[b_sq[0], b_const], writes=[b_ps[0]])
        P.op("act", lambda e: e.activation(out=rstd, in_=psum[0], func=AF.Sqrt, scale=1.0 / 128, bias=EPS),
             reads=[b_ps[0]], writes=[b_rstd])
        P.op("dve", lambda e: e.reciprocal(out=rstd, in_=rstd), reads=[b_rstd], writes=[b_rstd])
        P.op("dve", lambda e: e.scalar_tensor_tensor(out=RH[:, h, :], in0=scr[4], scalar=gsub08, in1=rstd,
                                                     op0=ALU.mult, op1=ALU.mult),
             reads=[b_scr[4], b_rstd, b_lam], writes=[b_RH[h]])

    def attn_da(s, i, h, q):
        steps = [(c, kb) for c in range(NCHK) for kb in range(KB)]
        slot_of = {}
        sbanks = [(4, 5), (6, 7)]

        def issue_S(n):
            c, kb = steps[n]
            if kb == 0:
                slot, bf, ch = ring_next()
                P.dma("sp", slot[:, 0:KCH], kv[s]["kda"][h, :, c * KCH:(c + 1) * KCH], reads=[b_kv[s]], writes=[bf],
                      chan=ch)
                P.dma("sp", slot[:, KCH:2 * KCH].rearrange("p (b d) -> p b d", d=128),
                      kv[s]["vda"][:, h, c * KB:(c + 1) * KB, :], reads=[b_kv[s]], writes=[bf], chan=ch)
                arow = (q * H + h) * 3
                for base in (0, 32):
                    P.dma("sp", slot[base:base + 3, 2 * KCH:3 * KCH], augk_b[arow:arow + 3, c * KCH:(c + 1) * KCH],
                          reads=[b_wb], writes=[bf], chan=ch)
                slot_of[c] = (slot, bf)
            slot, bf = slot_of[c]
            r = c * KB + kb
            col = r * H + h
            diag = 4 * i <= r < 4 * i + 4
            for m in range(2):
                pb = sbanks[n % 2][m]
                ti = 2 * (n % 2) + m
                P.op("pe", lambda e, m=m, pb=pb, slot=slot, kb=kb: e.matmul(
                    psum[pb], lhsT=slot[m * 64:(m + 1) * 64, kb * 128:(kb + 1) * 128],
                    rhs=RQ[m * 64:(m + 1) * 64, h, :], start=True, stop=diag),
                    reads=[bf, b_RQ[h]], writes=[b_ps[pb]])
            for m in range(2):
                pb = sbanks[n % 2][m]
                ti = 2 * (n % 2) + m
                if diag:
                    din = Dt[:, 1 + (r - 4 * i), :]
                    P.op("dve", lambda e, pb=pb, ti=ti, din=din, col=col: e.scalar_tensor_tensor(
                        out=tmpS[ti], in0=din, scalar=c1tab[:, col:col + 1], in1=psum[pb], op0=ALU.mult, op1=ALU.add),
                        reads=[b_ps[pb], b_Dt, b_tab], writes=[b_tmpS[ti]])
                    P.op("act", lambda e, ti=ti, col=col: e.activation(out=Pt[ti], in_=tmpS[ti], func=AF.Exp,
                                                                     scale=DA_SCALE, bias=btab[:, col:col + 1]),
                         reads=[b_tmpS[ti], b_tab], writes=[b_Pt[ti]])
                else:
                    P.op("pe", lambda e, m=m, pb=pb, slot=slot, kb=kb: e.matmul(
                        psum[pb], lhsT=slot[m * 32:m * 32 + 3, 2 * KCH + kb * 128:2 * KCH + (kb + 1) * 128],
                        rhs=qaug_t[m * 32:m * 32 + 3, :], start=False, stop=True),
                        reads=[bf, b_Dt], writes=[b_ps[pb]])
                    P.op("act", lambda e, ti=ti, col=col, pb=pb: e.activation(out=Pt[ti], in_=psum[pb], func=AF.Exp,
                                                                            scale=DA_SCALE, bias=btab[:, col:col + 1]),
                         reads=[b_ps[pb], b_tab], writes=[b_Pt[ti]])

        def issue_PV(n):
            c, kb = steps[n]
            slot, bf = slot_of[c]
            V = slot[:, KCH + kb * 128:KCH + (kb + 1) * 128]
            first, last = (n == 0), (n == len(steps) - 1)
            for m in range(2):
                ti = 2 * (n % 2) + m
                P.op("pe", lambda e, m=m, ti=ti, V=V: e.matmul(psum[m], lhsT=V, rhs=Pt[ti], start=first, stop=last),
                     reads=[bf, b_Pt[ti]], writes=[b_ps[m]])
                P.op("pe", lambda e, m=m, ti=ti: e.matmul(psum[2 + m], lhsT=ones_b, rhs=Pt[ti], start=first, stop=last),
                     reads=[b_const, b_Pt[ti]], writes=[b_ps[2 + m]])

        issue_S(0)
        for n in range(len(steps)):
            if n + 1 < len(steps):
                issue_S(n + 1)
            issue_PV(n)

    def attn_mla(s, i, h, hook):
        steps = [(c, kb) for c in range(NCHK) for kb in range(KB)]
        slot_of = {}

        def issue_S(n):
            c, kb = steps[n]
            if kb == 0:
                slot, bf, ch = ring_next()
                P.dma("sp", slot[:, 0:KCH], kv[s]["kn"][h, :, c * KCH:(c + 1) * KCH], reads=[b_kv[s]], writes=[bf],
                      chan=ch)
                P.dma("sp", slot[0:64, KCH:2 * KCH], kv[s]["kr"][:, c * KCH:(c + 1) * KCH], reads=[b_kv[s]],
                      writes=[bf], chan=ch)
                P.dma("sp", slot[:, 2 * KCH:3 * KCH].rearrange("p (b d) -> p b d", d=128),
                      kv[s]["vb"][:, h, c * KB:(c + 1) * KB, :], reads=[b_kv[s]], writes=[bf], chan=ch)
                slot_of[c] = (slot, bf)
            slot, bf = slot_of[c]
            pb = 4 + (n % 2)
            ti = 4 + (n % 2)
            P.op("pe", lambda e, pb=pb, slot=slot, kb=kb: e.matmul(
                psum[pb], lhsT=slot[:, kb * 128:(kb + 1) * 128], rhs=RQ[:, 8 + h, :], start=True, stop=False),
                reads=[bf, b_RQ[8 + h]], writes=[b_ps[pb]])
            P.op("pe", lambda e, pb=pb, slot=slot, kb=kb: e.matmul(
                psum[pb], lhsT=slot[0:64, KCH + kb * 128:KCH + (kb + 1) * 128], rhs=RQ[0:64, 16 + h, :],
                start=False, stop=True), reads=[bf, b_RQ[16 + h]], writes=[b_ps[pb]])
            P.op("act", lambda e, pb=pb, ti=ti: e.activation(out=Pt[ti], in_=psum[pb], func=AF.Exp, scale=MLA_SCALE),
                 reads=[b_ps[pb]], writes=[b_Pt[ti]])

        def issue_PV(n):
            c, kb = steps[n]
            slot, bf = slot_of[c]
            V = slot[:, 2 * KCH + kb * 128:2 * KCH + (kb + 1) * 128]
            first, last = (n == 0), (n == len(steps) - 1)
            ti = 4 + (n % 2)
            P.op("pe", lambda e, ti=ti, V=V: e.matmul(psum[6], lhsT=V, rhs=Pt[ti], start=first, stop=last),
                 reads=[bf, b_Pt[ti]], writes=[b_ps[6]])
            P.op("pe", lambda e, ti=ti: e.matmul(psum[7], lhsT=ones_b, rhs=Pt[ti], start=first, stop=last),
                 reads=[b_const, b_Pt[ti]], writes=[b_ps[7]])

        issue_S(0)
        for n in range(len(steps)):
            if n + 1 < len(steps):
                issue_S(n + 1)
            issue_PV(n)
            if n == min(2, len(steps) - 1):
                hook()
        P.op("dve", lambda e: e.reciprocal(out=scr[5], in_=psum[7]), reads=[b_ps[7]], writes=[b_scr[5]])
        P.op("dve", lambda e: e.tensor_tensor(out=RH[:, 8 + h, :], in0=psum[6], in1=scr[5], op=ALU.mult),
             reads=[b_ps[6], b_scr[5]], writes=[b_RH[8 + h]])

    def qgroup(s, i, q):
        rows = xkv[s][i * QG:(i + 1) * QG, :]
        load_xT(rows)
        P.dma("sp", c1tab, alibi[q, 0:1, :].partition_broadcast(128), writes=[b_tab], chan="tab")
        P.dma("sp", btab, alibi[q, 1:2, :].partition_broadcast(128), writes=[b_tab], chan="tab")
        P.dma("sp", ropeq, rope[s][:, :, i * QG:(i + 1) * QG].rearrange("a p t -> p a t"), writes=[b_ropeq],
              chan="ropeq")
        norm_stats(xT, b_xT, NCH, D, nb())
        norm_apply(xT, b_xT, gcol["norm_mix"], NCH, RH, b_RH)
        for tl in range(2):
            view, bf = wtile(wb_in[:, O_Q + tl * 512:O_Q + (tl + 1) * 512], NCH, 512)
            for cc in range(4):
                pb = nb()
                h = tl * 4 + cc
                for c in range(NCH):
                    P.op("pe", lambda e, c=c, cc=cc, pb=pb, view=view: e.matmul(
                        psum[pb], lhsT=view[:, c, cc * 128:(cc + 1) * 128], rhs=RH[:, c, :], start=(c == 0),
                        stop=(c == NCH - 1)), reads=[bf, b_RH[c]], writes=[b_ps[pb]])
                evac(RQ[:, h, :], psum[pb], [b_ps[pb]], [b_RQ[h]])
        view, bf = wtile(wb_in[:, O_CQ:O_CQ + 512], NCH, 512)
        for cc in range(4):
            pb = nb()
            for c in range(NCH):
                P.op("pe", lambda e, c=c, cc=cc, pb=pb, view=view: e.matmul(
                    psum[pb], lhsT=view[:, c, cc * 128:(cc + 1) * 128], rhs=RH[:, c, :], start=(c == 0),
                    stop=(c == NCH - 1)), reads=[bf, b_RH[c]], writes=[b_ps[pb]])
            evac(cqT[:, cc, :], psum[pb], [b_ps[pb]], [b_xin[0]])
        norm_stats(cqT, [b_xin[0]] * 4, 4, 512, nb())
        norm_apply(cqT, [b_xin[0]] * 4, gcol["mla_q_norm"], 4, cqn, [b_xin[1]] * 4, engs=("dve",))
        view, bf = wtile(wb_uqx, 4, 2048)
        for h in range(H):
            pb = nb()
            for c in range(4):
                P.op("pe", lambda e, c=c, h=h, pb=pb, view=view: e.matmul(
                    psum[pb], lhsT=view[:, c, h * 256:h * 256 + 128], rhs=cqn[:, c, :], start=(c == 0), stop=(c == 3)),
                    reads=[bf, b_xin[1]], writes=[b_ps[pb]])
            evac(RQ[:, 8 + h, :], psum[pb], [b_ps[pb]], [b_RQ[8 + h]])
            pbs = (nb(), nb())
            for j in range(2):
                for c in range(4):
                    P.op("pe", lambda e, c=c, h=h, j=j, pb=pbs[j], view=view: e.matmul(
                        psum[pb][0:64, :], lhsT=view[:, c, h * 256 + 128 + j * 64:h * 256 + 192 + j * 64],
                        rhs=cqn[:, c, :], start=(c == 0), stop=(c == 3)), reads=[bf, b_xin[1]], writes=[b_ps[pbs[j]]])
                P.op("dve", lambda e, j=j, pb=pbs[j]: e.tensor_tensor(out=qrt[:, j, :], in0=psum[pb][0:64, :],
                                                                     in1=ropeq[:, j, :], op=ALU.mult),
                     reads=[b_ps[pbs[j]], b_ropeq], writes=[b_xin[1]])
            P.op("dve", lambda e, h=h: e.tensor_tensor(out=RQ[0:64, 16 + h, :], in0=qrt[:, 0, :], in1=qrt[:, 1, :],
                                                       op=ALU.add), reads=[b_xin[1]], writes=[b_RQ[16 + h]])
        if cfg.debug2 and q == 0:
            P.dma("pool", dbg["RQ"], RQ.rearrange("p c t -> p (c t)"), reads=b_RQ, chan="dbg")
        if cfg.stop == 1:
            return
        pending = [None]
        for h in range(H):
            attn_da(s, i, h, q)
            attn_mla(s, i, h, (lambda h=h: da_epilogue(h)))
        if cfg.debug2 and q == 0:
            P.dma("pool", dbg["RH"], RH.rearrange("p c t -> p (c t)"), reads=b_RH, chan="dbg")
        if cfg.stop == 2:
            return
        norm_stats(xT, b_xT, NCH, D, nb())
        norm_apply(xT, b_xT, gcol["norm_mix"], NCH, RQ, b_RQ)
        for j in range(4):
            vga, bga = wtile(wb_in[:, O_GA + j * 512:O_GA + (j + 1) * 512], NCH, 512)
            vgb, bgb = wtile(wb_in[:, O_GB + j * 512:O_GB + (j + 1) * 512], NCH, 512)
            slot, bpp, ch = ring_next()
            vpa = slot[:, 0:4096].rearrange("p (c n) -> p c n", c=8)
            vpb = slot[:, 4096:8192].rearrange("p (c n) -> p c n", c=8)
            P.dma("sp", vpa, wb_pa[:, j * 512:(j + 1) * 512].rearrange("(c p) n -> p c n", p=128), reads=[b_wb],
                  writes=[bpp], chan=ch)
            P.dma("sp", vpb, wb_pb[:, j * 512:(j + 1) * 512].rearrange("(c p) n -> p c n", p=128), reads=[b_wb],
                  writes=[bpp], chan=ch)
            mb = 16 + (j % 2) * 4
            for cc in range(4):
                pa_, pb_, pc_, pd_ = nb(), nb(), nb(), nb()
                cs = slice(cc * 128, (cc + 1) * 128)
                for c in range(NCH):
                    P.op("pe", lambda e, c=c, cs=cs, pb=pa_, v=vga: e.matmul(
                        psum[pb], lhsT=v[:, c, cs], rhs=RQ[:, c, :], start=(c == 0), stop=(c == NCH - 1)),
                        reads=[bga, b_RQ[c]], writes=[b_ps[pa_]])
                for c in range(NCH):
                    P.op("pe", lambda e, c=c, cs=cs, pb=pb_, v=vgb: e.matmul(
                        psum[pb], lhsT=v[:, c, cs], rhs=RQ[:, c, :], start=(c == 0), stop=(c == NCH - 1)),
                        reads=[bgb, b_RQ[c]], writes=[b_ps[pb_]])
                for c in range(8):
                    P.op("pe", lambda e, c=c, cs=cs, pb=pc_, v=vpa: e.matmul(
                        psum[pb], lhsT=v[:, c, cs], rhs=RH[:, c, :], start=(c == 0), stop=(c == 7)),
                        reads=[bpp, b_RH[c]], writes=[b_ps[pc_]])
                for c in range(8):
                    P.op("pe", lambda e, c=c, cs=cs, pb=pd_, v=vpb: e.matmul(
                        psum[pb], lhsT=v[:, c, cs], rhs=RH[:, 8 + c, :], start=(c == 0), stop=(c == 7)),
                        reads=[bpp, b_RH[8 + c]], writes=[b_ps[pd_]])
                P.op("act", lambda e, pb=pa_: e.activation(out=scr[0], in_=psum[pb], func=AF.Sigmoid),
                     reads=[b_ps[pa_]], writes=[b_scr[0]])
                P.op("act", lambda e, pb=pb_: e.activation(out=scr[1], in_=psum[pb], func=AF.Sigmoid),
                     reads=[b_ps[pb_]], writes=[b_scr[1]])
                P.op("dve", lambda e, pb=pc_: e.tensor_tensor(out=scr[2], in0=psum[pb], in1=scr[0], op=ALU.mult),
                     reads=[b_ps[pc_], b_scr[0]], writes=[b_scr[2]])
                P.op("dve", lambda e, pb=pd_: e.tensor_tensor(out=scr[3], in0=psum[pb], in1=scr[1], op=ALU.mult),
                     reads=[b_ps[pd_], b_scr[1]], writes=[b_scr[3]])
                P.op("pool", lambda e, k=mb + cc: e.tensor_tensor(out=RQ[:, k, :], in0=scr[2], in1=scr[3], op=ALU.add),
                     reads=[b_scr[2], b_scr[3]], writes=[b_RQ[mb + cc]])
            vo, bo = wtile(wb_out[j * 512:(j + 1) * 512, :], 4, 2048)
            for oc in range(NCH):
                pb = nb()
                for c in range(4):
                    P.op("pe", lambda e, c=c, oc=oc, pb=pb, v=vo, mb=mb: e.matmul(
                        psum[pb], lhsT=v[:, c, oc * 128:(oc + 1) * 128], rhs=RQ[:, mb + c, :], start=(c == 0),
                        stop=(c == 3)), reads=[bo, b_RQ[mb + c]], writes=[b_ps[pb]])
                P.op("dve", lambda e, oc=oc, pb=pb: e.tensor_tensor(out=xT[:, oc, :], in0=xT[:, oc, :], in1=psum[pb],
                                                                   op=ALU.add),
                     reads=[b_ps[pb], b_xT[oc]], writes=[b_xT[oc]])
            if cfg.debug2 and q == 0:
                P.dma("pool", dbg["xj%d" % j], xT.rearrange("p c t -> p (c t)"), reads=b_xT, chan="dbg")
                P.dma("pool", dbg["mj%d" % j], RQ.rearrange("p c t -> p (c t)"), reads=b_RQ, chan="dbg")
        if cfg.debug2 and q == 0:
            P.dma("pool", dbg["x1"], xT.rearrange("p c t -> p (c t)"), reads=b_xT, chan="dbg")
            P.dma("pool", dbg["RQ2"], RQ.rearrange("p c t -> p (c t)"), reads=b_RQ, chan="dbg")
        if cfg.stop == 3:
            return
        norm_stats(xT, b_xT, NCH, D, nb())
        norm_apply(xT, b_xT, gcol["norm_ffn"], NCH, RQ, b_RQ)
        for j in range(NFJ):
            vg, bg = wtile(wb_gate[:, j * 512:(j + 1) * 512], NCH, 512)
            vu, bu = wtile(wb_up[:, j * 512:(j + 1) * 512], NCH, 512)
            ab = 16 + (j % 2) * 4
            for fc in range(4):
                pa_, pb_ = nb(), nb()
                cs = slice(fc * 128, (fc + 1) * 128)
                for c in range(NCH):
                    P.op("pe", lambda e, c=c, cs=cs, pb=pa_, v=vg: e.matmul(
                        psum[pb], lhsT=v[:, c, cs], rhs=RQ[:, c, :], start=(c == 0), stop=(c == NCH - 1)),
                        reads=[bg, b_RQ[c]], writes=[b_ps[pa_]])
                for c in range(NCH):
                    P.op("pe", lambda e, c=c, cs=cs, pb=pb_, v=vu: e.matmul(
                        psum[pb], lhsT=v[:, c, cs], rhs=RQ[:, c, :], start=(c == 0), stop=(c == NCH - 1)),
                        reads=[bu, b_RQ[c]], writes=[b_ps[pb_]])
                si = 4 + (fc % 2)
                P.op("act", lambda e, pb=pa_, si=si: e.activation(out=scr[si], in_=psum[pb], func=AF.Sigmoid),
                     reads=[b_ps[pa_]], writes=[b_scr[si]])
                P.op("dve", lambda e, pb=pa_, si=si: e.tensor_tensor(out=scr[si], in0=psum[pb], in1=scr[si], op=ALU.mult),
                     reads=[b_ps[pa_], b_scr[si]], writes=[b_scr[si]])
                P.op("dve", lambda e, pb=pb_, si=si, k=ab + fc: e.tensor_tensor(out=RQ[:, k, :], in0=psum[pb],
                                                                               in1=scr[si], op=ALU.mult),
                     reads=[b_ps[pb_], b_scr[si]], writes=[b_RQ[ab + fc]])
            vd, bd = wtile(wb_down[j * 512:(j + 1) * 512, :], 4, 2048)
            for oc in range(NCH):
                pb = nb()
                for c in range(4):
                    P.op("pe", lambda e, c=c, oc=oc, pb=pb, v=vd, ab=ab: e.matmul(
                        psum[pb], lhsT=v[:, c, oc * 128:(oc + 1) * 128], rhs=RQ[:, ab + c, :], start=(c == 0),
                        stop=(c == 3)), reads=[bd, b_RQ[ab + c]], writes=[b_ps[pb]])
                P.op("dve", lambda e, oc=oc, pb=pb: e.tensor_tensor(out=xT[:, oc, :], in0=xT[:, oc, :], in1=psum[pb],
                                                                   op=ALU.add),
                     reads=[b_ps[pb], b_xT[oc]], writes=[b_xT[oc]])
        if cfg.debug2 and q == 0:
            P.dma("pool", dbg["x2"], xT.rearrange("p c t -> p (c t)"), reads=b_xT, chan="dbg")
        if cfg.stop == 4:
            return
        norm_stats(xT, b_xT, NCH, D, nb())
        norm_apply(xT, b_xT, gcol["norm_final"], NCH, xT, b_xT)
        for b in range(4):
            bi = b % 2
            for c4 in range(NCH // 4):
                pb = nb()
                for k in range(4):
                    c = c4 * 4 + k
                    P.op("pe", lambda e, c=c, k=k, pb=pb, b=b: e.transpose(
                        out=psum[pb][:, k * 128:(k + 1) * 128], in_=xT[:, c, b * 128:(b + 1) * 128], identity=ident),
                        reads=[b_xT[c], b_const], writes=[b_ps[pb]])
                evac(xin[bi][:, c4 * 512:(c4 + 1) * 512], psum[pb], [b_ps[pb]], [b_xin[bi]])
            P.dma("pool", yout[s][i * QG + b * 128:i * QG + (b + 1) * 128, :], xin[bi], reads=[b_xin[bi]],
                  chan="yout%d" % bi)

    q = 0
    for s, nq in slots:
        for i in range(nq):
            qgroup(s, i, q)
            q += 1

    P.emit()
    return nc


def _rope_tables(S, off):
    inv = (10000.0 ** (-np.arange(0, 64, 2, dtype=np.float32) / np.float32(64))).astype(np.float32)
    pos = np.arange(S, dtype=np.float32)
    ang = (pos[:, None] * inv[None, :]).astype(np.float32)
    cos = np.concatenate([np.cos(ang), np.cos(ang)], axis=-1).astype(np.float32)
    sin = np.concatenate([np.sin(ang), np.sin(ang)], axis=-1).astype(np.float32)
    t = np.stack([cos.T, sin.T], axis=0)
    return np.ascontiguousarray(np.roll(t, -off, axis=2))


def _alibi_tables(S, off, nq):
    NKB = S // 128
    W = S - off
    slopes = 2.0 ** (-(np.arange(1, H + 1, dtype=np.float64)))
    out = np.zeros((nq, 2, NKB, H), np.float64)
    for i in range(nq):
        for r in range(NKB):
            if 4 * i <= r < 4 * i + 4:
                sig, offt = 1.0, 0.0
            elif r < 4 * i:
                sig, offt = 1.0, 512.0 * i - 128.0 * r
            elif 128 * r < W:
                sig, offt = -1.0, 128.0 * r - 512.0 * i
            else:
                sig, offt = 1.0, S + 512.0 * i - 128.0 * r
            out[i, 0, r, :] = -slopes * sig / DA_SCALE
            out[i, 1, r, :] = -slopes * offt
    return out.reshape(nq, 2, NKB * H).astype(np.float32)


def _augk_table(S, off, nq):
    NKB = S // 128
    W = S - off
    slopes = 2.0 ** (-(np.arange(1, H + 1, dtype=np.float64)))
    p = np.arange(128, dtype=np.float64)
    out = np.zeros((nq, H, 3, NKB, 128), np.float64)
    for i in range(nq):
        for r in range(NKB):
            if 4 * i <= r < 4 * i + 4:
                continue
            sig = 1.0 if (r < 4 * i or 128 * r >= W) else -1.0
            for h in range(H):
                c1 = -slopes[h] * sig / DA_SCALE
                out[i, h, 0, r, :] = c1
                out[i, h, 1, r, :] = c1
                out[i, h, 2, r, :] = -c1 * p
    return out.reshape(nq * H * 3, S).astype(np.float32)


def _qaug_table():
    qi = np.arange(QG, dtype=np.float32)
    lo = qi % 256
    return np.ascontiguousarray(np.stack([lo, qi - lo, np.ones_like(qi)], axis=0).astype(np.float32))


def _cst_table():
    p = np.arange(128, dtype=np.float32)[:, None]
    q = np.arange(512, dtype=np.float32)[None, :]
    parts = [np.eye(128, dtype=np.float32), q - p]
    for d in range(4):
        parts.append(np.abs(q - p - 128.0 * d))
    return np.ascontiguousarray(np.concatenate(parts, axis=1).astype(np.float32))


def make_in_map(cfg, w, xA, offA, xB, offB):
    S = cfg.S
    m = {
        "xkvA": np.ascontiguousarray(np.roll(xA, -offA, axis=0)),
        "xkvB": np.ascontiguousarray(np.roll(xB, -offB, axis=0)),
        "ropeA": _rope_tables(S, offA),
        "ropeB": _rope_tables(S, offB),
        "alibi": np.ascontiguousarray(np.concatenate([_alibi_tables(S, offA, cfg.NQA),
                                                      _alibi_tables(S, offB, cfg.NQB)], axis=0)),
        "cst": _cst_table(),
        "augk": np.ascontiguousarray(np.concatenate([_augk_table(S, offA, cfg.NQA),
                                                     _augk_table(S, offB, cfg.NQB)], axis=0)),
        "qaug": _qaug_table(),
        "w_in": w["w_in"][0], "w_uq": w["w_uq"][0], "w_ukv": w["w_ukv"][0],
        "w_proj_a": w["w_proj_a"][0], "w_proj_b": w["w_proj_b"][0], "w_out": w["w_out"][0],
        "w_gate": w["w_gate"][0], "w_up": w["w_up"][0], "w_down": w["w_down"][0],
        "norm_mix": w["norm_mix"].reshape(1, -1), "norm_ffn": w["norm_ffn"].reshape(1, -1),
        "norm_final": w["norm_final"].reshape(1, -1), "da_subln": w["da_subln"].reshape(1, -1),
        "mla_q_norm": w["mla_q_norm"].reshape(1, -1), "mla_kv_norm": w["mla_kv_norm"].reshape(1, -1),
        "lq1": w["da_lambda_q1"].reshape(1, -1), "lk1": w["da_lambda_k1"].reshape(1, -1),
        "lq2": w["da_lambda_q2"].reshape(1, -1), "lk2": w["da_lambda_k2"].reshape(1, -1),
    }
    return {k: np.ascontiguousarray(v, dtype=np.float32) for k, v in m.items()}


_NC_CACHE = {}


def _plan():
    plan = []
    for c in range(8):
        G0 = 12 * c
        first_seq, last_seq = G0 // 32, (G0 + 11) // 32
        if (G0 + 7) // 32 == first_seq:
            a = (first_seq, G0 % 32)
            b = ((G0 + 8) // 32, (G0 + 8) % 32)
        else:
            b = (first_seq, G0 % 32)
            a = ((G0 + 4) // 32, (G0 + 4) % 32)
        plan.append((a, b))
    return plan


def kernel(x_prompt, x_sample, norm_mix, w_in, da_lambda_q1, da_lambda_k1, da_lambda_q2, da_lambda_k2, da_subln,
           mla_q_norm, w_uq, mla_kv_norm, w_ukv, w_proj_a, w_proj_b, w_out, norm_ffn, w_gate, w_up, w_down,
           norm_final):
    w = dict(norm_mix=norm_mix, w_in=w_in, da_lambda_q1=da_lambda_q1, da_lambda_k1=da_lambda_k1,
             da_lambda_q2=da_lambda_q2, da_lambda_k2=da_lambda_k2, da_subln=da_subln, mla_q_norm=mla_q_norm,
             w_uq=w_uq, mla_kv_norm=mla_kv_norm, w_ukv=w_ukv, w_proj_a=w_proj_a, w_proj_b=w_proj_b, w_out=w_out,
             norm_ffn=norm_ffn, w_gate=w_gate, w_up=w_up, w_down=w_down, norm_final=norm_final)
    w = {k: np.asarray(v, dtype=np.float32) for k, v in w.items()}
    x_prompt = np.asarray(x_prompt, dtype=np.float32)
    x_sample = np.asarray(x_sample, dtype=np.float32)
    seqs = [x_prompt[0], x_prompt[1], x_sample[0]]
    S = seqs[0].shape[0]
    cfg = Cfg(S=S, NQA=8, NQB=4)
    if "nc" not in _NC_CACHE:
        _NC_CACHE["nc"] = build_nc(cfg)
    nc = _NC_CACHE["nc"]
    plan = _plan()
    in_maps = []
    for (sa, ga), (sb_, gb) in plan:
        in_maps.append(make_in_map(cfg, w, seqs[sa], ga * QG, seqs[sb_], gb * QG))
    res = run_bass_kernel_spmd(nc, in_maps, core_ids=list(range(8)))
    outs = [np.empty((S, D), np.float32) for _ in range(3)]
    for c, ((sa, ga), (sb_, gb)) in enumerate(plan):
        r = res.results[c]
        outs[sa][ga * QG:(ga + 8) * QG] = r["yA"]
        outs[sb_][gb * QG:(gb + 4) * QG] = r["yB"]
    y_prompt = np.stack([outs[0], outs[1]], axis=0)
    y_sample = outs[2][None]
    return (y_prompt, y_sample)
```
